# Optimizing a Trainium2 kernel written in Bass

```python
import jax, jax.numpy as jnp
from jax import lax
import numpy as np

D_MODEL = 1024
BATCH = 2
SEQ = 8192
DEPTH = 1

F_WIDTH = D_MODEL
F_GROUPS = 4
F_GROUP_DIM = F_WIDTH // F_GROUPS
M_WIDTH = 2 * D_MODEL
M_HEADS = 4
M_HEAD_DIM = M_WIDTH // M_HEADS
QKV_BLOCK = 4
CONV_K = 5
CHUNK = 128
N_BRANCHES = 2
EPS = 1e-6
IN_COLS = 2 * F_WIDTH + 3 * M_WIDTH + N_BRANCHES * D_MODEL

kernel_name = "hybrid_fnet_mlstm_gated_block"


def rmsnorm(x, w):
    xf = x.astype(jnp.float32)
    y = xf * lax.rsqrt(jnp.mean(xf * xf, axis=-1, keepdims=True) + EPS)
    return (y * w.astype(jnp.float32)).astype(x.dtype)


def fourier_mix(xf):
    b, s, _ = xf.shape
    xg = xf.astype(jnp.float32).reshape(b, s, F_GROUPS, F_GROUP_DIM)
    y = jnp.fft.fft2(xg, axes=(1, 3), norm="ortho").real
    return y.reshape(b, s, F_WIDTH).astype(xf.dtype)


def block_diag_proj(x, w):
    b, s, width = x.shape
    xb = x.reshape(b, s, width // QKV_BLOCK, QKV_BLOCK)
    return jnp.einsum('bsnc,ncd->bsnd', xb, w).reshape(b, s, width)


def centred_dwconv(x, w, bias):
    y = lax.conv_general_dilated(
        x, w[:, None, :], window_strides=(1,),
        padding=[(CONV_K // 2, CONV_K // 2)],
        dimension_numbers=('NWC', 'WIO', 'NWC'),
        feature_group_count=x.shape[-1])
    return y + bias


def mlstm_chunkwise(q, k, v, i_pre, log_f):
    b, h, s, d = q.shape
    nc = s // CHUNK

    def to_chunks(t):
        t = t.reshape(b, h, nc, CHUNK, *t.shape[3:])
        return jnp.moveaxis(t, 2, 0)

    xs = (to_chunks(q), to_chunks(k), to_chunks(v), to_chunks(i_pre), to_chunks(log_f))
    lower = jnp.tril(jnp.ones((CHUNK, CHUNK), dtype=bool))

    def step(carry, inp):
        c_st, n_st, m_st = carry
        qt, kt, vt, it, ft = inp
        bcum = jnp.cumsum(ft, axis=-1)
        dmat = bcum[..., :, None] - bcum[..., None, :] + it[..., None, :]
        dmat = jnp.where(lower, dmat, -jnp.inf)
        m_t = jnp.maximum(bcum + m_st[..., None], jnp.max(dmat, axis=-1))
        s_mat = jnp.einsum('bhtd,bhsd->bhts', qt, kt) * jnp.exp(dmat - m_t[..., None])
        inter = jnp.exp(bcum + m_st[..., None] - m_t)
        num = (jnp.einsum('bhts,bhsd->bhtd', s_mat, vt)
               + inter[..., None] * jnp.einsum('bhvk,bhtk->bhtv', c_st, qt))
        den = jnp.sum(s_mat, axis=-1) + inter * jnp.einsum('bhk,bhtk->bht', n_st, qt)
        h_out = num / jnp.maximum(jnp.abs(den), jnp.exp(-m_t))[..., None]
        g = bcum[..., -1]
        w_log = g[..., None] - bcum + it
        m_new = jnp.maximum(g + m_st, jnp.max(w_log, axis=-1))
        decay = jnp.exp(g + m_st - m_new)
        ws = jnp.exp(w_log - m_new[..., None])
        c_new = decay[..., None, None] * c_st + jnp.einsum('bhs,bhsv,bhsk->bhvk', ws, vt, kt)
        n_new = decay[..., None] * n_st + jnp.einsum('bhs,bhsk->bhk', ws, kt)
        return (c_new, n_new, m_new), h_out

    init = (jnp.zeros((b, h, d, d), jnp.float32),
            jnp.zeros((b, h, d), jnp.float32),
            jnp.zeros((b, h), jnp.float32))
    _, hs = lax.scan(step, init, xs)
    return jnp.moveaxis(hs, 0, 2).reshape(b, h, s, d)


def setup_inputs(seed: int = 0) -> dict:
    key = jax.random.key(seed)
    ks = jax.random.split(key, 24)

    def nrm(k, shape, scale):
        return jax.random.normal(k, shape, jnp.float32) * scale

    gate_in = 3 * M_WIDTH
    f_bias = jnp.broadcast_to(jnp.linspace(3.0, 6.0, M_HEADS, dtype=jnp.float32), (DEPTH, M_HEADS))
    nb = M_WIDTH // QKV_BLOCK
    return {
        "x": nrm(ks[0], (BATCH, SEQ, D_MODEL), 1.0),
        "norm_w": 1.0 + nrm(ks[1], (DEPTH, D_MODEL), 0.02),
        "w_in": nrm(ks[2], (DEPTH, D_MODEL, IN_COLS), D_MODEL ** -0.5),
        "w_fourier": nrm(ks[3], (DEPTH, F_WIDTH, D_MODEL), F_WIDTH ** -0.5),
        "conv_w": nrm(ks[4], (DEPTH, CONV_K, M_WIDTH), CONV_K ** -0.5),
        "conv_b": nrm(ks[5], (DEPTH, M_WIDTH), 0.02),
        "w_q": nrm(ks[6], (DEPTH, nb, QKV_BLOCK, QKV_BLOCK), QKV_BLOCK ** -0.5),
        "w_k": nrm(ks[7], (DEPTH, nb, QKV_BLOCK, QKV_BLOCK), QKV_BLOCK ** -0.5),
        "w_v": nrm(ks[8], (DEPTH, nb, QKV_BLOCK, QKV_BLOCK), QKV_BLOCK ** -0.5),
        "w_igate_fwd": nrm(ks[9], (DEPTH, gate_in, M_HEADS), 0.1 * gate_in ** -0.5),
        "b_igate_fwd": nrm(ks[10], (DEPTH, M_HEADS), 0.1),
        "w_fgate_fwd": nrm(ks[11], (DEPTH, gate_in, M_HEADS), 0.1 * gate_in ** -0.5),
        "b_fgate_fwd": f_bias + nrm(ks[12], (DEPTH, M_HEADS), 0.01),
        "w_igate_bwd": nrm(ks[13], (DEPTH, gate_in, M_HEADS), 0.1 * gate_in ** -0.5),
        "b_igate_bwd": nrm(ks[14], (DEPTH, M_HEADS), 0.1),
        "w_fgate_bwd": nrm(ks[15], (DEPTH, gate_in, M_HEADS), 0.1 * gate_in ** -0.5),
        "b_fgate_bwd": f_bias + nrm(ks[16], (DEPTH, M_HEADS), 0.01),
        "hnorm_w": 1.0 + nrm(ks[17], (DEPTH, M_WIDTH), 0.02),
        "skip_w": 1.0 + nrm(ks[18], (DEPTH, M_WIDTH), 0.02),
        "w_mlstm": nrm(ks[19], (DEPTH, M_WIDTH, D_MODEL), M_WIDTH ** -0.5),
        "w_out": nrm(ks[20], (DEPTH, D_MODEL, D_MODEL), D_MODEL ** -0.5),
        "final_norm_w": 1.0 + nrm(ks[21], (D_MODEL,), 0.02),
    }


def reference(x, norm_w, w_in, w_fourier, conv_w, conv_b, w_q, w_k, w_v,
              w_igate_fwd, b_igate_fwd, w_fgate_fwd, b_fgate_fwd,
              w_igate_bwd, b_igate_bwd, w_fgate_bwd, b_fgate_bwd,
              hnorm_w, skip_w, w_mlstm, w_out, final_norm_w):
    bsz, seq, _ = x.shape
    split_points = np.cumsum([F_WIDTH, F_WIDTH, M_WIDTH, M_WIDTH, M_WIDTH]).tolist()

    def to_heads(t):
        return jnp.transpose(t.reshape(bsz, seq, M_HEADS, M_HEAD_DIM), (0, 2, 1, 3)).astype(jnp.float32)

    for l in range(DEPTH):
        h = rmsnorm(x, norm_w[l])
        proj = h @ w_in[l]
        x_f, z_f, x_m, z_m, o_m, g_logits = jnp.split(proj, split_points, axis=-1)

        y_a = (fourier_mix(x_f) * jax.nn.silu(z_f)) @ w_fourier[l]

        x_c = jax.nn.silu(centred_dwconv(x_m, conv_w[l], conv_b[l]))
        q = block_diag_proj(x_c, w_q[l])
        k = block_diag_proj(x_c, w_k[l])
        v = block_diag_proj(x_m, w_v[l])
        qkv = jnp.concatenate([q, k, v], axis=-1)
        i_fwd = jnp.transpose(qkv @ w_igate_fwd[l] + b_igate_fwd[l], (0, 2, 1)).astype(jnp.float32)
        f_fwd = jax.nn.log_sigmoid(jnp.transpose(qkv @ w_fgate_fwd[l] + b_fgate_fwd[l], (0, 2, 1)).astype(jnp.float32))
        i_bwd = jnp.transpose(qkv @ w_igate_bwd[l] + b_igate_bwd[l], (0, 2, 1)).astype(jnp.float32)
        f_bwd = jax.nn.log_sigmoid(jnp.transpose(qkv @ w_fgate_bwd[l] + b_fgate_bwd[l], (0, 2, 1)).astype(jnp.float32))
        qh = to_heads(q) * (M_HEAD_DIM ** -0.5)
        kh = to_heads(k)
        vh = to_heads(v)
        h_fwd = mlstm_chunkwise(qh, kh, vh, i_fwd, f_fwd)
        h_bwd = jnp.flip(mlstm_chunkwise(jnp.flip(qh, 2), jnp.flip(kh, 2), jnp.flip(vh, 2),
                                         jnp.flip(i_bwd, 2), jnp.flip(f_bwd, 2)), 2)
        h_cell = jnp.transpose(h_fwd + h_bwd, (0, 2, 1, 3))
        h_cell = h_cell * jax.nn.sigmoid(o_m.astype(jnp.float32)).reshape(bsz, seq, M_HEADS, M_HEAD_DIM)
        h_cell = h_cell * lax.rsqrt(jnp.mean(h_cell * h_cell, axis=-1, keepdims=True) + EPS)
        h_cell = h_cell.reshape(bsz, seq, M_WIDTH).astype(x.dtype) * hnorm_w[l]
        y_b = ((h_cell + skip_w[l] * x_c) * jax.nn.silu(z_m)) @ w_mlstm[l]

        g_a, g_b = jnp.split(jax.nn.sigmoid(g_logits), N_BRANCHES, axis=-1)
        x = x + (g_a * y_a + g_b * y_b) @ w_out[l]

    return rmsnorm(x, final_norm_w)
```

```python
import numpy as np
import ml_dtypes
from contextlib import ExitStack

import concourse.bass as bass
import concourse.mybir as mybir
from concourse.bass_utils import run_bass_kernel_spmd

F32 = mybir.dt.float32
BF16 = mybir.dt.bfloat16
I32 = mybir.dt.int32
ALU = mybir.AluOpType
AF = mybir.ActivationFunctionType
AX = mybir.AxisListType

ENGS = ["tensor", "vector", "scalar", "gpsimd", "sync"]
S = 8192
D = 1024
NCH = 64
EPS = 1e-6


class Buf:
    __slots__ = ("name", "lw", "rd")

    def __init__(self, name):
        self.name = name
        self.lw = None
        self.rd = {}


class Prog:
    def __init__(self, nc, stack, n_dma_sems=24):
        self.nc = nc
        self.q = {e: [] for e in ENGS}
        self.cnt = {e: 0 for e in ENGS}
        self.waited = {e: {} for e in ENGS}
        self.semobj = {}
        for e in ENGS:
            self.semobj[("e", e)] = stack.enter_context(nc.semaphore("se_" + e))
        self.dsem_keys = []
        self.dval = {}
        for i in range(n_dma_sems):
            k = ("d", i)
            self.semobj[k] = stack.enter_context(nc.semaphore("sd_%d" % i))
            self.dsem_keys.append(k)
            self.dval[k] = 0
        self.ck = ("c", 0)
        self.semobj[self.ck] = stack.enter_context(nc.semaphore("s_cc"))
        self.cval = 0
        self.dnext = 0

    def _deps(self, reads, writes, skip_key=None):
        deps = {}

        def add(k, v):
            if k == skip_key:
                return
            if deps.get(k, 0) < v:
                deps[k] = v

        for b in reads:
            if b.lw is not None:
                add(*b.lw)
        for b in writes:
            if b.lw is not None:
                add(*b.lw)
            for k, v in b.rd.items():
                add(k, v)
        return deps

    def _emit_waits(self, eng, deps):
        w = self.waited[eng]
        for k, v in deps.items():
            if w.get(k, 0) < v:
                w[k] = v
                sem = self.semobj[k]
                self.q[eng].append(lambda e, sem=sem, v=v: e.wait_ge(sem, v))

    def _mark(self, reads, writes, k, v):
        for b in reads:
            if b.rd.get(k, 0) < v:
                b.rd[k] = v
        for b in writes:
            b.lw = (k, v)
            b.rd = {}

    def op(self, eng, fn, reads=(), writes=()):
        k = ("e", eng)
        deps = self._deps(reads, writes, skip_key=k if eng == "tensor" else None)
        self._emit_waits(eng, deps)
        self.cnt[eng] += 1
        v = self.cnt[eng]
        sem = self.semobj[k]
        self.q[eng].append(lambda e, fn=fn, sem=sem: fn(e).then_inc(sem, 1))
        self._mark(reads, writes, k, v)

    def mm(self, fns, reads=(), writes=()):
        k = ("e", "tensor")
        deps = self._deps(reads, writes, skip_key=k)
        self._emit_waits("tensor", deps)
        self.cnt["tensor"] += 1
        v = self.cnt["tensor"]
        sem = self.semobj[k]
        for fn in fns[:-1]:
            self.q["tensor"].append(lambda e, fn=fn: fn(e))
        self.q["tensor"].append(lambda e, fn=fns[-1], sem=sem: fn(e).then_inc(sem, 1))
        self._mark(reads, writes, k, v)

    def dma(self, out, in_, reads=(), writes=(), eng="sync", fn=None, **kw):
        k = self.dsem_keys[self.dnext % len(self.dsem_keys)]
        self.dnext += 1
        deps = self._deps(reads, writes)
        if self.dval[k] > 0 and deps.get(k, 0) < self.dval[k]:
            deps[k] = self.dval[k]
        self._emit_waits(eng, deps)
        self.dval[k] += 16
        v = self.dval[k]
        sem = self.semobj[k]
        if fn is None:
            self.q[eng].append(
                lambda e, out=out, in_=in_, sem=sem, kw=kw: e.dma_start(out=out, in_=in_, **kw).then_inc(sem, 16))
        else:
            self.q[eng].append(lambda e, fn=fn, sem=sem: fn(e).then_inc(sem, 16))
        self._mark(reads, writes, k, v)

    def coll(self, fn, reads=(), writes=()):
        deps = self._deps(reads, writes)
        self._emit_waits("gpsimd", deps)
        self.cval += 1
        sem = self.semobj[self.ck]
        self.q["gpsimd"].append(lambda e, fn=fn, sem=sem: fn(e).then_inc(sem, 1))
        self._mark(reads, writes, self.ck, self.cval)

    def raw(self, eng, fn):
        self.q[eng].append(fn)

    def barrier(self, coll=True):
        deps = {}
        for e in ENGS:
            if self.cnt[e] > 0:
                deps[("e", e)] = self.cnt[e]
        for k in self.dsem_keys:
            if self.dval[k] > 0:
                deps[k] = self.dval[k]
        if self.cval > 0 and coll:
            deps[self.ck] = self.cval
        for e in ENGS:
            self._emit_waits(e, dict(deps))

    def run(self):
        nc = self.nc
        q = self.q
        with nc.Block() as block:
            @block.tensor
            def _(e):
                for f in q["tensor"]:
                    f(e)

            @block.vector
            def _(e):
                for f in q["vector"]:
                    f(e)

            @block.scalar
            def _(e):
                for f in q["scalar"]:
                    f(e)

            @block.gpsimd
            def _(e):
                for f in q["gpsimd"]:
                    f(e)

            @block.sync
            def _(e):
                for f in q["sync"]:
                    f(e)


GROUPS = [[0, 1, 2, 3], [4, 5, 6, 7]]


def build_program(dbg=False, stop_after=99, skip1=False, nocoll=False, parts="abg", tiny=False):
    nc = bass.Bass("TRN2", target_bir_lowering=False)

    def din(name, shape, dt=F32):
        if tiny and name in (("xT", "xTq", "xq", "wg", "wfo", "wml", "wout") if stop_after <= 3 else ("xT",)):
            shape = [2, 16]
        return nc.dram_tensor("i_" + name, shape, dt, kind="ExternalInput").ap()

    def dscr(name, shape, dt=BF16, ext=False):
        if ext and dbg:
            return nc.dram_tensor(name, shape, dt, kind="ExternalOutput").ap()
        return nc.dram_tensor(name, shape, dt).ap()

    xT_d = din("xT", [D, S])
    xTq_d = din("xTq", [D, 2048])
    xq_d = din("xq", [2048, D])
    wown_d = din("w_own", [D, 1792])
    wfT_d = din("wfT", [256, D])
    wg_d = din("wg", [D, 2048])
    nw_d = din("nw", [128, 8])
    dftc_d = din("dftc", [128, 2, 512], BF16)
    convw_d = din("convw", [128, 4, 5])
    convb_d = din("convb", [128, 4])
    bd_d = din("bd", [128, 3, 4, 128])
    bdT_d = din("bdT", [128, 3, 4, 128])
    wgate_d = din("wgate", [128, 12, 16])
    sel_d = din("sel", [64, 4])
    gb_d = din("gb", [64, 4])
    hnw_d = din("hnw", [128, 4])
    skw_d = din("skw", [128, 4])
    wfo_d = din("wfo", [D, D])
    wml_d = din("wml", [2048, D])
    wout_d = din("wout", [D, D])
    fnw_d = din("fnw", [1, D])
    qoff_d = din("qoff", [1, 2], I32)
    m1_d = din("m1", [64, 128, 384], BF16)
    cs2_d = din("cs2", [128, 64], BF16)
    identb_d = din("identb", [128, 128], BF16)
    identf_d = din("identf", [128, 128])
    mask_d = din("masks", [128, 2, 128])
    ustr_d = din("ustr", [64, 2, 64])
    out_d = nc.dram_tensor("out", [2048, D], F32, kind="ExternalOutput").ap()

    Zs_d = dscr("Zs", [S, 512], BF16, ext=True)
    A1_d = dscr("A1s", [64, 128, 512], BF16)
    szfT_d = dscr("szfT", [256, S], BF16, ext=True)
    szmT_d = dscr("szmT", [NCH, 128, 4, 128], BF16, ext=True)
    xcT_d = dscr("xcT", [NCH, 128, 4, 128], BF16, ext=True)
    qT_d = dscr("qT", [NCH, 128, 4, 128], BF16, ext=True)
    kT_d = dscr("kT", [NCH, 128, 4, 128], BF16, ext=True)
    ktm_d = dscr("ktm", [S, 512], BF16, ext=True)
    vtm_d = dscr("vtm", [S, 512], BF16, ext=True)
    sigo_d = dscr("sigo", [S, 512], BF16, ext=True)
    gp_d = dscr("gp", [16, S], F32)
    gr_d = dscr("gr", [16, S], F32)
    arow_d = dscr("arow", [2, S], F32)
    hst_d = dscr("hst", [2, NCH, 128, 512], BF16)
    uin_d = dscr("uin", [16, 512, 512], BF16)
    uout_d = dscr("uout", [16, 2048, 512], BF16)
    ufin_d = dscr("ufin", [8, 256, 1024], BF16)
    ufout_d = dscr("ufout", [8, 1024, 1024], BF16)
    if dbg:
        dbg_gr = nc.dram_tensor("dbg_gr", [16, S], F32, kind="ExternalOutput").ap()
        dbg_cols = nc.dram_tensor("dbg_cols", [2, 128, 5, 64], F32, kind="ExternalOutput").ap()
        dbg_uown = nc.dram_tensor("dbg_uown", [16, 512, 512], BF16, kind="ExternalOutput").ap()
        dbg_h = nc.dram_tensor("dbg_h", [2, NCH, 128, 512], BF16, kind="ExternalOutput").ap()

    B = {}

    def bf(name):
        if name not in B:
            B[name] = Buf(name)
        return B[name]

    with ExitStack() as top:
        P = Prog(nc, top)

        def sbt(st, name, shape, dt=F32):
            return st.enter_context(nc.sbuf_tensor(name, shape, dt))

        def pst(st, name, shape, dt=F32):
            return st.enter_context(nc.psum_tensor(name, shape, dt))

        identb = sbt(top, "identb", [128, 128], BF16)
        identf = sbt(top, "identf", [128, 128], F32)
        onesb = sbt(top, "onesb", [128, 128], BF16)
        onesf = sbt(top, "onesf", [128, 128], F32)
        nw = sbt(top, "nw", [128, 8], F32)
        P.dma(identb[:], identb_d, writes=[bf("identb")])
        P.dma(identf[:], identf_d, writes=[bf("identf")])
        P.dma(nw[:], nw_d, writes=[bf("nw")])
        hnw = sbt(top, "hnw", [128, 4], F32)
        skw = sbt(top, "skw", [128, 4], F32)
        P.dma(hnw[:], hnw_d, writes=[bf("hnw")])
        P.dma(skw[:], skw_d, writes=[bf("skw")])
        P.op("vector", lambda e: e.memset(onesb[:], 1.0), writes=[bf("onesb")])
        P.op("vector", lambda e: e.memset(onesf[:], 1.0), writes=[bf("onesf")])
        epsc = sbt(top, "epsc", [128, 1], F32)
        P.op("vector", lambda e: e.memset(epsc[:], EPS), writes=[bf("epsc")])

        def xload(src_d, t0, xs, tag):
            src = src_d.rearrange("(k p) t -> p k t", p=128)[:, :, t0:t0 + 512]
            P.dma(xs[:], src, writes=[bf("xs" + tag)])

        def xblock(src_d, t0, xs, xb, sq, Rbc, Rcol, ps_st, ps_sc, tag, want_col, after_cast=None):
            P.op("scalar", lambda e: e.activation(out=xb[:], in_=xs[:], func=AF.Copy),
                 reads=[bf("xs" + tag)], writes=[bf(xb.name)])
            P.op("gpsimd", lambda e: e.tensor_tensor(sq[:], xs[:], xs[:], ALU.mult),
                 reads=[bf("xs" + tag)], writes=[bf("sq" + tag)])
            if after_cast is not None:
                after_cast()
            P.mm([(lambda e, k=k: e.matmul(ps_st[:], onesb[:], sq[:, k, :], start=(k == 0), stop=(k == 7)))
                  for k in range(8)],
                 reads=[bf("onesb"), bf("sq" + tag)], writes=[bf(ps_st.name)])
            P.op("scalar", lambda e: e.activation(out=Rbc[:], in_=ps_st[:], func=AF.Sqrt, bias=EPS, scale=1.0 / D),
                 reads=[bf(ps_st.name)], writes=[bf(Rbc.name)])
            P.op("vector", lambda e: e.reciprocal(Rbc[:], Rbc[:]), reads=[bf(Rbc.name)], writes=[bf(Rbc.name)])
            if want_col:
                P.mm([(lambda e, tt=tt: e.matmul(ps_sc[:, tt:tt + 1], Rbc[0:1, tt * 128:(tt + 1) * 128],
                                                  onesf[0:1, 0:1], start=True, stop=True)) for tt in range(4)],
                     reads=[bf(Rbc.name), bf("onesf")], writes=[bf(ps_sc.name)])
                P.op("vector", lambda e: e.tensor_copy(Rcol[:], ps_sc[:, 0:4]),
                     reads=[bf(ps_sc.name)], writes=[bf(Rcol.name)])

        with ExitStack() as ph1:
            Wb = sbt(ph1, "Wb", [128, 8, 1792], BF16)
            Wfz = sbt(ph1, "Wfz", [128, 8, 512], BF16)
            wstage = sbt(ph1, "wstage", [128, 1792], F32)
            bdb = sbt(ph1, "bdb", [128, 3, 4, 128], BF16)
            diagW = sbt(ph1, "diagW", [128, 4, 5, 128], BF16)
            Wgc = sbt(ph1, "Wgc", [128, 4, 16], BF16)
            Wgm = sbt(ph1, "Wgm", [128, 4, 16], BF16)
            convw = sbt(ph1, "convw", [128, 4, 5], F32)
            convb = sbt(ph1, "convb", [128, 4], F32)
            xs = sbt(ph1, "xs", [128, 8, 512], F32)
            xb = [sbt(ph1, "xb%d" % i, [128, 8, 512], BF16) for i in range(2)]
            sq = sbt(ph1, "sq", [128, 8, 512], BF16)
            Rbc = [sbt(ph1, "Rbc%d" % i, [128, 512], F32) for i in range(2)]
            Rcol = [sbt(ph1, "Rcol%d" % i, [128, 4], F32) for i in range(2)]
            tmpf = [sbt(ph1, "tmpf%d" % i, [128, 512], F32) for i in range(2)]
            szf_o = sbt(ph1, "szf_o", [128, 2, 512], BF16)
            szm_o = [sbt(ph1, "szm_o%d" % i, [128, 4, 4, 128], BF16) for i in range(2)]
            w1_o = sbt(ph1, "w1_o", [128, 4, 4, 128], BF16)
            xmw = [sbt(ph1, "xmw%d" % i, [128, 4, 516], BF16) for i in range(3)]
            z_o = sbt(ph1, "z_o", [128, 4, 512], BF16)
            so_o = sbt(ph1, "so_o", [128, 4, 512], BF16)
            xc = sbt(ph1, "xc", [128, 4, 512], BF16)
            xc_o = sbt(ph1, "xc_o", [128, 4, 4, 128], BF16)
            q_o = sbt(ph1, "q_o", [128, 4, 4, 128], BF16)
            k_o = sbt(ph1, "k_o", [128, 4, 4, 128], BF16)
            ktm_o = sbt(ph1, "ktm_o", [128, 4, 512], BF16)
            vtm_o = sbt(ph1, "vtm_o", [128, 4, 512], BF16)
            gp_o = sbt(ph1, "gp_o", [16, 512], F32)

            ps_pj = [pst(ph1, "ps_pj%d" % i, [128, 512]) for i in range(2)]
            ps_tm = [pst(ph1, "ps_tm%d" % i, [128, 512]) for i in range(2)]
            ps_st = pst(ph1, "ps_st", [128, 512])
            ps_cv = [pst(ph1, "ps_cv%d" % i, [128, 512]) for i in range(2)]
            ps_sm = pst(ph1, "ps_sm", [128, 512])

            with ExitStack() as ph0:
                wfTs = sbt(ph0, "wfTs", [128, 2, D], F32)
                wfTb = sbt(ph0, "wfTb", [128, 2, D], BF16)
                dftc = sbt(ph0, "dftc", [128, 2, 512], BF16)
                bds = sbt(ph0, "bds", [128, 3, 4, 128], F32)
                bdTs = sbt(ph0, "bdTs", [128, 3, 4, 128], F32)
                wgate = sbt(ph0, "wgate", [128, 12, 16], F32)

                for kc in range(8):
                    P.dma(wstage[:], wown_d[kc * 128:(kc + 1) * 128, :], writes=[bf("wstage")])
                    P.op("vector", lambda e, kc=kc: e.tensor_scalar(Wb[:, kc, :], wstage[:], nw[:, kc:kc + 1], None,
                                                                     ALU.mult),
                         reads=[bf("wstage"), bf("nw")], writes=[bf("Wb")])
                P.dma(wfTs[:], wfT_d.rearrange("(j p) d -> p j d", p=128), writes=[bf("wfTs")])
                P.dma(dftc[:], dftc_d, writes=[bf("dftc")])
                P.op("scalar", lambda e: e.activation(out=wfTb[:], in_=wfTs[:], func=AF.Copy),
                     reads=[bf("wfTs")], writes=[bf("wfTb")])
                for dc in range(8):
                    pp = ps_pj[dc % 2]
                    P.mm([(lambda e, j=j, dc=dc, pp=pp: e.matmul(pp[:], wfTb[:, j, dc * 128:(dc + 1) * 128],
                                                                  dftc[:, j, :], start=(j == 0), stop=(j == 1)))
                          for j in range(2)],
                         reads=[bf("wfTb"), bf("dftc")], writes=[bf(pp.name)])
                    P.op("vector", lambda e, dc=dc, pp=pp: e.tensor_scalar(Wfz[:, dc, :], pp[:], nw[:, dc:dc + 1],
                                                                            None, ALU.mult),
                         reads=[bf(pp.name), bf("nw")], writes=[bf("Wfz")])
                P.dma(bds[:], bd_d, writes=[bf("bds")])
                P.dma(bdTs[:], bdT_d, writes=[bf("bdTs")])
                P.dma(wgate[:], wgate_d, writes=[bf("wgate")])
                P.dma(convw[:], convw_d, writes=[bf("convw")])
                P.dma(convb[:], convb_d, writes=[bf("convb")])
                P.op("scalar", lambda e: e.activation(out=bdb[:], in_=bds[:], func=AF.Copy),
                     reads=[bf("bds")], writes=[bf("bdb")])
                for j in range(4):
                    for tap in range(5):
                        P.op("vector", lambda e, j=j, tap=tap: e.tensor_scalar(
                            diagW[:, j, tap, :], identf[:], convw[:, j, tap:tap + 1], None, ALU.mult),
                            reads=[bf("identf"), bf("convw")], writes=[bf("diagW")])
                for j in range(4):
                    P.mm([lambda e, j=j: e.matmul(ps_sm[:, 0:16], bdTs[:, 0, j, :], wgate[:, j, :], start=True,
                                                   stop=False),
                          lambda e, j=j: e.matmul(ps_sm[:, 0:16], bdTs[:, 1, j, :], wgate[:, 4 + j, :], start=False,
                                                   stop=True),
                          lambda e, j=j: e.matmul(ps_sm[:, 16:32], bdTs[:, 2, j, :], wgate[:, 8 + j, :], start=True,
                                                   stop=True)],
                         reads=[bf("bdTs"), bf("wgate")], writes=[bf("ps_sm")])
                    P.op("vector", lambda e, j=j: e.tensor_copy(Wgc[:, j, :], ps_sm[:, 0:16]),
                         reads=[bf("ps_sm")], writes=[bf("Wgc")])
                    P.op("vector", lambda e, j=j: e.tensor_copy(Wgm[:, j, :], ps_sm[:, 16:32]),
                         reads=[bf("ps_sm")], writes=[bf("Wgm")])
                P.barrier()

            NB = 16
            NBR = 0 if skip1 else NB
            pj_i = [0]
            tm_i = [0]
            cv_i = [0]
            tf_i = [0]

            def conv_block(tb):
                win = xmw[tb % 3]
                wname = win.name
                for j in range(4):
                    pp = ps_cv[cv_i[0] % 2]
                    cv_i[0] += 1
                    P.mm([(lambda e, j=j, tap=tap, pp=pp: e.matmul(pp[:], diagW[:, j, tap, :],
                                                                    win[:, j, tap:tap + 512],
                                                                    start=(tap == 0), stop=(tap == 4)))
                          for tap in range(5)],
                         reads=[bf("diagW"), bf(wname)], writes=[bf(pp.name)])
                    P.op("scalar", lambda e, j=j, pp=pp: e.activation(out=xc[:, j, :], in_=pp[:], func=AF.Silu,
                                                                      bias=convb[:, j:j + 1], scale=1.0),
                         reads=[bf(pp.name), bf("convb")], writes=[bf("xc")])
                P.op("gpsimd", lambda e: e.tensor_tensor(xc_o[:].rearrange("p c j t -> p j c t"),
                                                         xc[:].rearrange("p j (c t) -> p j c t", t=128),
                                                         skw[:, :, None, None].broadcast_to([128, 4, 4, 128]),
                                                         ALU.mult),
                     reads=[bf("xc"), bf("skw")], writes=[bf("xc_o")])
                P.op("gpsimd", lambda e, so=szm_o[tb % 2]: e.tensor_tensor(xc_o[:], xc_o[:], so[:], ALU.mult),
                     reads=[bf("xc_o"), bf(szm_o[tb % 2].name)], writes=[bf("xc_o")])
                P.dma(xcT_d[tb * 4:(tb + 1) * 4].rearrange("c p j t -> p c (j t)"),
                      xc_o[:].rearrange("p c j t -> p c (j t)"), reads=[bf("xc_o")], writes=[bf("xcT_d")])
                for which, dst, dd, scale in ((0, q_o, qT_d, 512.0 ** -0.5), (1, k_o, kT_d, 1.0)):
                    for j in range(4):
                        pp = ps_cv[cv_i[0] % 2]
                        cv_i[0] += 1
                        P.mm([lambda e, j=j, pp=pp, which=which: e.matmul(pp[:], bdb[:, which, j, :], xc[:, j, :],
                                                                           start=True, stop=True)],
                             reads=[bf("bdb"), bf("xc")], writes=[bf(pp.name)])
                        P.op("scalar" if j % 2 == 0 else "vector",
                             (lambda e, j=j, pp=pp, dst=dst, scale=scale:
                              e.activation(out=dst[:, :, j, :], in_=pp[:].rearrange("p (c t) -> p c t", t=128),
                                           func=AF.Copy, scale=scale)) if j % 2 == 0 else
                             (lambda e, j=j, pp=pp, dst=dst, scale=scale:
                              e.tensor_scalar(dst[:, :, j, :], pp[:].rearrange("p (c t) -> p c t", t=128),
                                              scale, None, ALU.mult)),
                             reads=[bf(pp.name)], writes=[bf(dst.name)])
                    P.dma(dd[tb * 4:(tb + 1) * 4].rearrange("c p j t -> p c (j t)"),
                          dst[:].rearrange("p c j t -> p c (j t)"), reads=[bf(dst.name)], writes=[bf(dst.name + "_d")])
                for which, dst, dd, srcname in ((1, ktm_o, ktm_d, "xc"), (2, vtm_o, vtm_d, wname)):
                    for tt in range(4):
                        pp = ps_tm[tm_i[0] % 2]
                        tm_i[0] += 1
                        if which == 1:
                            fns = [(lambda e, j=j, tt=tt, pp=pp: e.matmul(
                                pp[:, j * 128:(j + 1) * 128], xc[:, j, tt * 128:(tt + 1) * 128], bdb[:, 1, j, :],
                                start=True, stop=True)) for j in range(4)]
                        else:
                            fns = [(lambda e, j=j, tt=tt, pp=pp: e.matmul(
                                pp[:, j * 128:(j + 1) * 128], win[:, j, 2 + tt * 128:2 + (tt + 1) * 128],
                                bdb[:, 2, j, :], start=True, stop=True)) for j in range(4)]
                        P.mm(fns, reads=[bf("bdb"), bf(srcname)], writes=[bf(pp.name)])
                        if tt % 2 == 0:
                            P.op("scalar", lambda e, tt=tt, pp=pp, dst=dst: e.activation(out=dst[:, tt, :], in_=pp[:],
                                                                                         func=AF.Copy),
                                 reads=[bf(pp.name)], writes=[bf(dst.name)])
                        else:
                            P.op("vector", lambda e, tt=tt, pp=pp, dst=dst: e.tensor_copy(dst[:, tt, :], pp[:]),
                                 reads=[bf(pp.name)], writes=[bf(dst.name)])
                    P.dma(dd[tb * 512:(tb + 1) * 512, :].rearrange("(tt p) c -> p tt c", p=128), dst[:],
                          reads=[bf(dst.name)], writes=[bf(dst.name + "_d")])
                P.mm([(lambda e, j=j: e.matmul(ps_sm[0:16, :], Wgc[:, j, :], xc[:, j, :], start=(j == 0), stop=False))
                      for j in range(4)] +
                     [(lambda e, j=j: e.matmul(ps_sm[0:16, :], Wgm[:, j, :], win[:, j, 2:514], start=False,
                                               stop=(j == 3))) for j in range(4)],
                     reads=[bf("Wgc"), bf("Wgm"), bf("xc"), bf(wname)], writes=[bf("ps_sm")])
                P.op("vector", lambda e: e.tensor_copy(gp_o[:], ps_sm[0:16, :]), reads=[bf("ps_sm")],
                     writes=[bf("gp_o")])
                P.dma(gp_d[:, tb * 512:(tb + 1) * 512], gp_o[:], reads=[bf("gp_o")], writes=[bf("gp_d")])

            for tb in range(NBR):
                X = xb[tb % 2]
                R = Rbc[tb % 2]
                RC = Rcol[tb % 2]
                if tb == 0:
                    xload(xT_d, 0, xs, "")
                xblock(xT_d, tb * 512, xs, X, sq, R, RC, ps_st, ps_sm, "", True,
                       after_cast=(lambda tb=tb: xload(xT_d, (tb + 1) * 512, xs, "")) if tb + 1 < NBR else None)
                win = xmw[tb % 3]
                wname = win.name
                def fm_unit(cc):
                    pp = ps_pj[pj_i[0] % 2]
                    pj_i[0] += 1
                    P.mm([(lambda e, k=k, cc=cc, pp=pp, X=X: e.matmul(pp[:], Wb[:, k, cc * 128:(cc + 1) * 128],
                                                                       X[:, k, :], start=(k == 0), stop=(k == 7)))
                          for k in range(8)],
                         reads=[bf("Wb"), bf(X.name)], writes=[bf(pp.name)])
                    if cc < 2 or cc >= 6:
                        tf = tmpf[tf_i[0] % 2]
                        tf_i[0] += 1
                        P.op("vector", lambda e, pp=pp, tf=tf, R=R: e.tensor_tensor(tf[:], pp[:], R[:], ALU.mult),
                             reads=[bf(pp.name), bf(R.name)], writes=[bf(tf.name)])
                        if cc < 2:
                            P.op("scalar", lambda e, cc=cc, tf=tf: e.activation(out=szf_o[:, cc, :], in_=tf[:],
                                                                                func=AF.Silu),
                                 reads=[bf(tf.name)], writes=[bf("szf_o")])
                        else:
                            j = cc - 6
                            P.op("scalar", lambda e, j=j, tf=tf, so=szm_o[tb % 2]: e.activation(
                                out=so[:, :, j, :], in_=tf[:].rearrange("p (c t) -> p c t", t=128), func=AF.Silu),
                                reads=[bf(tf.name)], writes=[bf(szm_o[tb % 2].name)])
                    else:
                        j = cc - 2
                        P.op("vector", lambda e, j=j, pp=pp, R=R, win=win: e.tensor_tensor(
                            win[:, j, 2:514], pp[:], R[:], ALU.mult),
                            reads=[bf(pp.name), bf(R.name)], writes=[bf(wname)])
                def tm_unit(tt, which):
                    pp = ps_tm[tm_i[0] % 2]
                    tm_i[0] += 1
                    if which == 0:
                        P.mm([(lambda e, k=k, tt=tt, pp=pp, X=X: e.matmul(
                            pp[:], X[:, k, tt * 128:(tt + 1) * 128], Wfz[:, k, :], start=(k == 0), stop=(k == 7)))
                            for k in range(8)],
                            reads=[bf("Wfz"), bf(X.name)], writes=[bf(pp.name)])
                        P.op("scalar", lambda e, tt=tt, pp=pp, RC=RC: e.activation(
                            out=z_o[:, tt, :], in_=pp[:], func=AF.Copy, scale=RC[:, tt:tt + 1]),
                            reads=[bf(pp.name), bf(RC.name)], writes=[bf("z_o")])
                    else:
                        P.mm([(lambda e, k=k, tt=tt, pp=pp, X=X: e.matmul(
                            pp[:], X[:, k, tt * 128:(tt + 1) * 128], Wb[:, k, 1280:1792], start=(k == 0),
                            stop=(k == 7))) for k in range(8)],
                            reads=[bf("Wb"), bf(X.name)], writes=[bf(pp.name)])
                        P.op("scalar", lambda e, tt=tt, pp=pp, RC=RC: e.activation(
                            out=so_o[:, tt, :], in_=pp[:], func=AF.Sigmoid, scale=RC[:, tt:tt + 1]),
                            reads=[bf(pp.name), bf(RC.name)], writes=[bf("so_o")])
                order = []
                for i in range(8):
                    order.append(("f", i))
                    order.append(("t", i))
                order += [("f", 8), ("f", 9)]
                for kind, i in order:
                    if kind == "f":
                        fm_unit(i)
                    else:
                        tm_unit(i // 2, i % 2)
                P.dma(szfT_d.rearrange("(j p) t -> p j t", p=128)[:, :, tb * 512:(tb + 1) * 512], szf_o[:],
                      reads=[bf("szf_o")], writes=[bf("szfT_d")])
                P.op("gpsimd", lambda e, so=szm_o[tb % 2]: e.tensor_tensor(
                    w1_o[:], so[:], hnw[:, None, :, None].broadcast_to([128, 4, 4, 128]), ALU.mult),
                    reads=[bf(szm_o[tb % 2].name), bf("hnw")], writes=[bf("w1_o")])
                P.dma(szmT_d[tb * 4:(tb + 1) * 4].rearrange("c p j t -> p c (j t)"),
                      w1_o[:].rearrange("p c j t -> p c (j t)"), reads=[bf("w1_o")], writes=[bf("szmT_d")])
                if tb == 0:
                    P.op("gpsimd", lambda e, win=win: e.memset(win[:, :, 0:2], 0.0), writes=[bf(wname)])
                else:
                    prev = xmw[(tb - 1) % 3]
                    P.op("gpsimd", lambda e, win=win, prev=prev: e.tensor_copy(win[:, :, 0:2], prev[:, :, 512:514]),
                         reads=[bf(prev.name)], writes=[bf(wname)])
                    P.op("gpsimd", lambda e, win=win, prev=prev: e.tensor_copy(prev[:, :, 514:516], win[:, :, 2:4]),
                         reads=[bf(wname)], writes=[bf(prev.name)])
                if tb == NB - 1:
                    P.op("gpsimd", lambda e, win=win: e.memset(win[:, :, 514:516], 0.0), writes=[bf(wname)])
                P.dma(Zs_d[tb * 512:(tb + 1) * 512, :].rearrange("(tt p) c -> p tt c", p=128), z_o[:],
                      reads=[bf("z_o")], writes=[bf("Zs_d")])
                P.dma(sigo_d[tb * 512:(tb + 1) * 512, :].rearrange("(tt p) c -> p tt c", p=128), so_o[:],
                      reads=[bf("so_o")], writes=[bf("sigo_d")])
                if tb >= 1:
                    conv_block(tb - 1)
            if not skip1:
                conv_block(NB - 1)
            P.barrier()
        if stop_after <= 1:
            P.run()
            return nc

        def L(*names):
            return [bf(n) for n in names]

        def V(fn, r, w):
            P.op("vector", fn, reads=L(*r), writes=L(*w))

        def A(fn, r, w):
            P.op("scalar", fn, reads=L(*r), writes=L(*w))

        def G(fn, r, w):
            P.op("gpsimd", fn, reads=L(*r), writes=L(*w))

        with ExitStack() as ph2:
            Abc = [sbt(ph2, "Abc%d" % d, [128, S], F32) for d in range(2)]
            cols = [sbt(ph2, "cols%d" % d, [128, 4, 64], F32) for d in range(2)]
            dbc = [sbt(ph2, "dbc%d" % d, [128, 64], F32) for d in range(2)]
            ubc = [sbt(ph2, "ubc%d" % d, [128, 64], F32) for d in range(2)]
            masks = sbt(ph2, "masks", [128, 2, 128], F32)
            P.dma(masks[:], mask_d, writes=L("masks"))

            with ExitStack() as phf:
                if nocoll:
                    P.dma(gr_d, gp_d, reads=L("gp_d"), writes=L("gr_d"))
                else:
                    P.coll(lambda e: e.collective_compute("AllReduce", ALU.add, replica_groups=GROUPS,
                                                          ins=[gp_d.opt()], outs=[gr_d.opt()]),
                           reads=L("gp_d"), writes=L("gr_d"))
                if dbg:
                    P.dma(dbg_gr, gr_d, reads=L("gr_d"), writes=L("dbg_gr"))
                Zt = [sbt(phf, "Zt%d" % i, [128, 512], BF16) for i in range(3)]
                M1t = [sbt(phf, "M1t%d" % i, [128, 384], BF16) for i in range(3)]
                A1o = [sbt(phf, "A1o%d" % i, [128, 512], BF16) for i in range(2)]
                szf = sbt(phf, "szf", [128, 2, S], BF16)
                ufT = sbt(phf, "ufT", [128, 2, S], BF16)
                T2 = [sbt(phf, "T2_%d" % i, [128, 8, 256], BF16) for i in range(2)]
                cs2 = sbt(phf, "cs2", [128, 64], BF16)
                ps_f = [pst(phf, "ps_f%d" % i, [128, 512]) for i in range(2)]
                ps_y = [pst(phf, "ps_y%d" % i, [128, 512]) for i in range(2)]
                pg = pst(phf, "pg", [128, 512])
                pcol = pst(phf, "pcol", [128, 512])
                P.dma(cs2[:], cs2_d, writes=L("cs2"))
                P.dma(szf[:], szfT_d.rearrange("(j p) t -> p j t", p=128), reads=L("szfT_d"), writes=L("szf"))
                Zv = Zs_d.rearrange("(n1 n2) c -> n2 n1 c", n2=64)

                def f1_load(n2):
                    zt = Zt[n2 % 3]
                    mt = M1t[n2 % 3]
                    P.dma(zt[:], Zv[n2], reads=L("Zs_d"), writes=L(zt.name))
                    P.dma(mt[:], m1_d[n2], writes=L(mt.name))

                def f1_unit(n2):
                    zt = Zt[n2 % 3]
                    mt = M1t[n2 % 3]
                    pp = ps_f[n2 % 2]
                    ao = A1o[n2 % 2]
                    P.mm([lambda e: e.matmul(pp[:, 0:256], mt[:, 0:128], zt[:, 0:256], start=True, stop=False),
                          lambda e: e.matmul(pp[:, 0:256], mt[:, 128:256], zt[:, 256:512], start=False, stop=True),
                          lambda e: e.matmul(pp[:, 256:512], mt[:, 0:128], zt[:, 256:512], start=True, stop=False),
                          lambda e: e.matmul(pp[:, 256:512], mt[:, 256:384], zt[:, 0:256], start=False, stop=True)],
                         reads=L(zt.name, mt.name), writes=L(pp.name))
                    if n2 % 2 == 0:
                        A(lambda e: e.activation(out=ao[:], in_=pp[:], func=AF.Copy), [pp.name], [ao.name])
                    else:
                        V(lambda e: e.tensor_copy(ao[:], pp[:]), [pp.name], [ao.name])
                    P.dma(A1_d[n2], ao[:], reads=L(ao.name), writes=L("A1_d"))

                def f2_load(kb):
                    t2 = T2[kb % 2]
                    P.dma(t2[0:64, :, :], A1_d[:, kb * 8:(kb + 1) * 8, 0:256], reads=L("A1_d"), writes=L(t2.name))
                    P.dma(t2[64:128, :, :], A1_d[:, kb * 8:(kb + 1) * 8, 256:512], reads=L("A1_d"),
                          writes=L(t2.name))

                def f2_unit(kb):
                    t2 = T2[kb % 2]
                    for hh in range(2):
                        pp = ps_y[hh]
                        P.mm([(lambda e, j=j, pp=pp, hh=hh: e.matmul(pp[:, j * 64:(j + 1) * 64],
                                                                     t2[:, j, hh * 128:(hh + 1) * 128],
                                                                     cs2[:], start=True, stop=True)) for j in range(8)],
                             reads=L(t2.name, "cs2"), writes=L(pp.name))
                        ov = ufT[:, hh, :].rearrange("p (k2 k1) -> p k1 k2", k1=128)[:, kb * 8:(kb + 1) * 8, :]
                        sv = szf[:, hh, :].rearrange("p (k2 k1) -> p k1 k2", k1=128)[:, kb * 8:(kb + 1) * 8, :]
                        V(lambda e, ov=ov, sv=sv, pp=pp: e.tensor_tensor(
                            ov, pp[:].rearrange("p (j k) -> p j k", k=64), sv, ALU.mult),
                          [pp.name, "szf"], ["ufT"])

                Gt = sbt(phf, "Gt", [64, 16, 128], F32)
                sel = sbt(phf, "sel", [64, 4], F32)
                gb = sbt(phf, "gb", [64, 4], F32)
                ngb = sbt(phf, "ngb", [64, 4], F32)
                ustr = sbt(phf, "ustr", [64, 2, 64], F32)
                gt = {}
                for d in range(2):
                    for nm_ in ["I", "F", "E", "Lg", "Sg", "Bn", "a", "Mx", "Ag", "inter", "ws", "EM", "drep", "urep"]:
                        gt[(nm_, d)] = sbt(phf, "g%s%d" % (nm_, d), [64, 128], F32)
                    for nm_, shp in [("Tc", [64, 1]), ("Eo", [64, 1]), ("mxc", [64, 1]), ("mrow", [1, 64]),
                                     ("vrow", [1, 64]), ("urow", [1, 64]), ("uc", [64, 2]), ("nun", [64, 1]),
                                     ("dec", [64, 1])]:
                        gt[(nm_, d)] = sbt(phf, "g%s%d" % (nm_, d), shp, F32)

                def gates_units(d):
                    fwd = (d == 0)
                    T = lambda n: gt[(n, d)]
                    N = lambda n: gt[(n, d)].name

                    def rv(ap):
                        return ap if fwd else ap[:, ::-1]

                    def u0():
                        for dst, t in ((T("I"), 2 * d), (T("F"), 2 * d + 1)):
                            V(lambda e, dst=dst, t=t: e.tensor_scalar(dst[:], Gt[:, 4 * t, :], sel[:, 0:1], None,
                                                                      ALU.mult), ["Gt", "sel"], [dst.name])
                            for h in range(1, 4):
                                V(lambda e, dst=dst, t=t, h=h: e.scalar_tensor_tensor(
                                    dst[:], Gt[:, 4 * t + h, :], sel[:, h:h + 1], dst[:], ALU.mult, ALU.add),
                                  ["Gt", "sel", dst.name], [dst.name])
                        A(lambda e: e.activation(out=T("I")[:], in_=T("I")[:], func=AF.Identity,
                                                 bias=gb[:, 2 * d:2 * d + 1], scale=1.0), [N("I"), "gb"], [N("I")])
                        A(lambda e: e.activation(out=T("E")[:], in_=T("F")[:], func=AF.Exp,
                                                 bias=ngb[:, 2 * d + 1:2 * d + 2], scale=-1.0),
                          [N("F"), "ngb"], [N("E")])
                        A(lambda e: e.activation(out=T("Lg")[:], in_=T("E")[:], func=AF.Ln, bias=1.0, scale=1.0),
                          [N("E")], [N("Lg")])
                        V(lambda e: e.tensor_tensor_scan(rv(T("Sg")[:]), rv(onesf[0:64, :]), rv(T("Lg")[:]), 0.0,
                                                         ALU.mult, ALU.add), [N("Lg"), "onesf"], [N("Sg")])
                        V(lambda e: e.tensor_copy(T("Tc")[:], T("Sg")[:, 127:128] if fwd else T("Sg")[:, 0:1]),
                          [N("Sg")], [N("Tc")])
                        P.mm([lambda e: e.matmul(pg[0:64, 0:1], ustr[:, d, :], T("Tc")[:], start=True, stop=True)],
                             reads=L("ustr", N("Tc")), writes=L("pg"))

                    def u1():
                        V(lambda e: e.tensor_copy(T("Eo")[:], pg[0:64, 0:1]), ["pg"], [N("Eo")])
                        V(lambda e: e.tensor_scalar(T("Bn")[:], T("Sg")[:], T("Eo")[:, 0:1], None, ALU.add),
                          [N("Sg"), N("Eo")], [N("Bn")])
                        V(lambda e: e.tensor_tensor(T("a")[:], T("I")[:], T("Bn")[:], ALU.add),
                          [N("I"), N("Bn")], [N("a")])
                        V(lambda e: e.tensor_tensor_scan(rv(T("Mx")[:]), rv(onesf[0:64, :]), rv(T("a")[:]), 0.0,
                                                         ALU.mult, ALU.max), [N("a"), "onesf"], [N("Mx")])
                        V(lambda e: e.tensor_copy(T("mxc")[:], T("Mx")[:, 127:128] if fwd else T("Mx")[:, 0:1]),
                          [N("Mx")], [N("mxc")])
                        P.mm([lambda e: e.matmul(pg[0:1, 64:128], T("mxc")[:], identf[0:64, 0:64], start=True,
                                                 stop=True)], reads=L("identf", N("mxc")), writes=L("pg"))

                    def u2():
                        V(lambda e: e.tensor_copy(T("mrow")[:], pg[0:1, 64:128]), ["pg"], [N("mrow")])
                        V(lambda e: e.tensor_tensor_scan(rv(T("vrow")[:]), rv(onesf[0:1, 0:64]), rv(T("mrow")[:]), 0.0,
                                                         ALU.mult, ALU.max), [N("mrow"), "onesf"], [N("vrow")])
                        V(lambda e: e.memset(T("urow")[:], 0.0), [], [N("urow")])
                        if fwd:
                            V(lambda e: e.tensor_copy(T("urow")[:, 1:64], T("vrow")[:, 0:63]), [N("vrow"), N("urow")],
                              [N("urow")])
                        else:
                            V(lambda e: e.tensor_copy(T("urow")[:, 0:63], T("vrow")[:, 1:64]), [N("vrow"), N("urow")],
                              [N("urow")])
                        P.mm([lambda e: e.matmul(pg[0:64, 130:131], T("urow")[:], onesf[0:1, 0:1], start=True,
                                                 stop=True),
                              lambda e: e.matmul(pg[0:64, 131:132], T("vrow")[:], onesf[0:1, 0:1], start=True,
                                                 stop=True)],
                             reads=L("onesf", N("urow"), N("vrow")), writes=L("pg"))

                    def u3():
                        V(lambda e: e.tensor_copy(T("uc")[:], pg[0:64, 130:132]), ["pg"], [N("uc")])
                        V(lambda e: e.tensor_scalar(T("Ag")[:], T("Mx")[:], T("uc")[:, 0:1], None, ALU.max),
                          [N("Mx"), N("uc")], [N("Ag")])
                        V(lambda e: e.tensor_scalar(T("nun")[:], T("uc")[:, 1:2], -1.0, None, ALU.mult),
                          [N("uc")], [N("nun")])
                        A(lambda e: e.activation(out=T("inter")[:], in_=T("Ag")[:], func=AF.Exp,
                                                 bias=T("uc")[:, 0:1], scale=-1.0), [N("Ag"), N("uc")], [N("inter")])
                        A(lambda e: e.activation(out=T("ws")[:], in_=T("a")[:], func=AF.Exp, bias=T("nun")[:, 0:1],
                                                 scale=1.0), [N("a"), N("nun")], [N("ws")])
                        A(lambda e: e.activation(out=T("dec")[:], in_=T("uc")[:, 0:1], func=AF.Exp,
                                                 bias=T("nun")[:, 0:1], scale=1.0), [N("uc"), N("nun")], [N("dec")])
                        V(lambda e: e.tensor_tensor(T("EM")[:], T("Bn")[:], T("Ag")[:], ALU.subtract),
                          [N("Bn"), N("Ag")], [N("EM")])
                        A(lambda e: e.activation(out=T("EM")[:], in_=T("EM")[:], func=AF.Exp), [N("EM")], [N("EM")])
                        V(lambda e: e.tensor_scalar(T("drep")[:], onesf[0:64, :], T("dec")[:, 0:1], None, ALU.mult),
                          ["onesf", N("dec")], [N("drep")])
                        V(lambda e: e.tensor_scalar(T("urep")[:], onesf[0:64, :], T("uc")[:, 0:1], None, ALU.mult),
                          ["onesf", N("uc")], [N("urep")])
                        P.mm([(lambda e, qi=qi, nm_=nm_: e.matmul(pcol[:, qi * 64:(qi + 1) * 64], T(nm_)[:],
                                                                   identf[0:64, 0:64], start=True, stop=True))
                              for qi, nm_ in enumerate(["a", "ws", "inter", "EM", "drep", "urep"])],
                             reads=L("identf", N("a"), N("ws"), N("inter"), N("EM"), N("drep"), N("urep")),
                             writes=L("pcol"))
                        P.dma(arow_d[d:d + 1, :].rearrange("o (c p) -> (o c) p", p=128), T("Ag")[:],
                              reads=L(N("Ag")), writes=L("arow_d%d" % d))

                    def u4():
                        V(lambda e: e.tensor_copy(cols[d][:].rearrange("p q c -> p (q c)"), pcol[:, 0:256]),
                          ["pcol"], [cols[d].name])
                        V(lambda e: e.tensor_copy(dbc[d][:], pcol[:, 256:320]), ["pcol"], [dbc[d].name])
                        V(lambda e: e.tensor_copy(ubc[d][:], pcol[:, 320:384]), ["pcol"], [ubc[d].name])
                        P.dma(Abc[d][:], arow_d[d:d + 1, :].partition_broadcast(128), reads=L("arow_d%d" % d),
                              writes=L(Abc[d].name))
                        if dbg:
                            P.dma(dbg_cols[d, :, 0:4, :], cols[d][:], reads=L(cols[d].name), writes=L("dbg_cols"))
                            P.dma(dbg_cols[d, :, 4, :], dbc[d][:], reads=L(dbc[d].name), writes=L("dbg_cols"))

                    return [u0, u1, u2, u3, u4]

                NF1 = 64 if "a" in parts else 0
                for n2 in range(min(2, NF1)):
                    f1_load(n2)
                for n2 in range(NF1):
                    if n2 + 2 < NF1:
                        f1_load(n2 + 2)
                    f1_unit(n2)
                P.dma(Gt[:], gr_d.rearrange("g (c p) -> c g p", p=128), reads=L("gr_d"), writes=L("Gt"))
                P.dma(sel[:], sel_d, writes=L("sel"))
                P.dma(gb[:], gb_d, writes=L("gb"))
                P.dma(ustr[:], ustr_d, writes=L("ustr"))
                V(lambda e: e.tensor_scalar(ngb[:], gb[:], -1.0, None, ALU.mult), ["gb"], ["ngb"])
                gu = (gates_units(0) + gates_units(1)) if "g" in parts else []
                gi = 0
                NF2 = 16 if "b" in parts else 0
                if NF2:
                    f2_load(0)
                for kb in range(NF2):
                    if gi < len(gu):
                        gu[gi]()
                        gi += 1
                    if kb + 1 < NF2:
                        f2_load(kb + 1)
                    f2_unit(kb)
                while gi < len(gu):
                    gu[gi]()
                    gi += 1
                for j in range(2):
                    P.dma(ufin_d[:, j * 128:(j + 1) * 128, :].rearrange("tg p t -> p tg t"),
                          ufT[:, j, :].rearrange("p (tg t) -> p tg t", t=1024), reads=L("ufT"),
                          writes=L(*["ufin%d" % tg for tg in range(8)]))
                for tg in range(8):
                    if nocoll:
                        for r in range(4):
                            P.dma(ufout_d[tg, r * 256:(r + 1) * 256, :], ufin_d[tg], reads=L("ufin%d" % tg),
                                  writes=L("ufout%d" % tg))
                    else:
                        P.coll(lambda e, tg=tg: e.collective_compute(
                            "AllGather", ALU.bypass, replica_groups=GROUPS, ins=[ufin_d[tg].opt()],
                            outs=[ufout_d[tg].opt()]), reads=L("ufin%d" % tg), writes=L("ufout%d" % tg))
                P.barrier(coll=False)
            if stop_after <= 2:
                if dbg:
                    P.barrier()
                P.run()
                return nc

            with ExitStack() as phm:
                D2 = range(2)
                Cbf = [sbt(phm, "Cbf%d" % d, [128, 4, 512], BF16) for d in D2]
                n32 = [sbt(phm, "n32_%d" % d, [128, 4], F32) for d in D2]
                nbf = [sbt(phm, "nbf%d" % d, [128, 4], BF16) for d in D2]

                def two(name, shape, dt, n=2):
                    return [[sbt(phm, "%s%d_%d" % (name, d, i), shape, dt) for i in range(n)] for d in D2]

                def one(name, shape, dt):
                    return [sbt(phm, "%s%d" % (name, d), shape, dt) for d in D2]

                qTc = two("qTc", [128, 4, 128], BF16, 3)
                kTc = two("kTc", [128, 4, 128], BF16, 3)
                ktc = two("ktc", [128, 512], BF16, 3)
                vtc = two("vtc", [128, 512], BF16, 3)
                hoth = two("hoth", [128, 512], BF16, 3)
                sgc = two("sgc", [128, 512], BF16, 3)
                xcc = two("xcc", [128, 4, 128], BF16, 3)
                szc = two("szc", [128, 4, 128], BF16, 3)
                hbuf = two("hbuf", [128, 512], BF16, 3)
                u_o = two("u_o", [128, 4, 128], BF16)
                tmpD = one("tmpD", [128, 128], F32)
                Dm = one("Dm", [128, 128], F32)
                ibc = one("ibc", [128, 128], BF16)
                qs = one("qs", [128, 4, 128], BF16)
                PT = one("PT", [128, 128], BF16)
                kw = two("kw", [128, 512], BF16)
                decI = two("decI", [128, 128], BF16)
                den = one("den", [128, 2], F32)
                rr = one("rr", [128, 1], F32)
                hs = one("hs", [128, 512], F32)
                junk = one("junk", [128, 512], BF16)
                ssq = one("ssq", [128, 1], F32)
                hn = one("hn", [128, 512], BF16)
                e2 = one("e2", [128, 4, 128], F32)
                e3 = one("e3", [128, 4, 128], F32)
                ps_misc = pst(phm, "ps_misc", [128, 512])
                ps_tr = pst(phm, "ps_tr", [128, 1024], BF16)
                ps_num = [pst(phm, "ps_num%d" % d, [128, 512]) for d in D2]
                ps_dc = [[pst(phm, "ps_dc%d_%d" % (d, i), [128, 512]) for i in range(2)] for d in D2]

                for d in D2:
                    V(lambda e, d=d: e.memset(Cbf[d][:], 0.0), [], ["Cbf%d_%d" % (d, j) for j in range(4)])
                    V(lambda e, d=d: e.memset(n32[d][:], 0.0), [], [n32[d].name])
                    V(lambda e, d=d: e.memset(nbf[d][:], 0.0), [], [nbf[d].name])

                def load_step(d, c, it):
                    sl = it % 3
                    P.dma(qTc[d][sl][:].rearrange("p j t -> p (j t)"), qT_d[c].rearrange("p j t -> p (j t)"),
                          reads=L("q_o_d"), writes=L(qTc[d][sl].name))
                    P.dma(kTc[d][sl][:].rearrange("p j t -> p (j t)"), kT_d[c].rearrange("p j t -> p (j t)"),
                          reads=L("k_o_d"), writes=L(kTc[d][sl].name))
                    P.dma(ktc[d][sl][:], ktm_d[c * 128:(c + 1) * 128, :], reads=L("ktm_o_d"),
                          writes=L(ktc[d][sl].name))
                    P.dma(vtc[d][sl][:], vtm_d[c * 128:(c + 1) * 128, :], reads=L("vtm_o_d"),
                          writes=L(vtc[d][sl].name))
                    P.dma(sgc[d][sl][:], sigo_d[c * 128:(c + 1) * 128, :], reads=L("sigo_d"),
                          writes=L(sgc[d][sl].name))

                def load_ep(d, c, it):
                    if it >= 32:
                        s3 = it % 3
                        P.dma(hoth[d][s3][:], hst_d[1 - d, c], reads=L("hst_d%d" % (1 - d)),
                              writes=L(hoth[d][s3].name))
                        P.dma(xcc[d][s3][:].rearrange("p j t -> p (j t)"), xcT_d[c].rearrange("p j t -> p (j t)"),
                              reads=L("xcT_d"), writes=L(xcc[d][s3].name))
                        P.dma(szc[d][s3][:].rearrange("p j t -> p (j t)"), szmT_d[c].rearrange("p j t -> p (j t)"),
                              reads=L("szmT_d"), writes=L(szc[d][s3].name))

                def exchange(tg):
                    if nocoll:
                        for r in range(4):
                            P.dma(uout_d[tg, r * 512:(r + 1) * 512, :], uin_d[tg], reads=L("uin%d" % tg),
                                  writes=L("uout%d" % tg))
                    else:
                        P.coll(lambda e: e.collective_compute("AllGather", ALU.bypass, replica_groups=GROUPS,
                                                              ins=[uin_d[tg].opt()], outs=[uout_d[tg].opt()]),
                               reads=L("uin%d" % tg), writes=L("uout%d" % tg))

                def CBn(d):
                    return ["Cbf%d_%d" % (d, j) for j in range(4)]

                def stage1(d, c, it):
                    sl = it % 3
                    qT, kT, kt = qTc[d][sl], kTc[d][sl], ktc[d][sl]
                    o = d * 256
                    P.mm([(lambda e, j=j: e.matmul(ps_misc[:, o:o + 128], kT[:, j, :], qT[:, j, :], start=(j == 0),
                                                   stop=(j == 3))) for j in range(4)],
                         reads=L(kT.name, qT.name), writes=L("ps_misc"))
                    V(lambda e: e.scalar_tensor_tensor(tmpD[d][:], Abc[d][:, c * 128:(c + 1) * 128],
                                                       cols[d][:, 0, c:c + 1], masks[:, d, :], ALU.subtract, ALU.max),
                      [Abc[d].name, cols[d].name, "masks"], [tmpD[d].name])
                    A(lambda e: e.activation(out=Dm[d][:], in_=tmpD[d][:], func=AF.Exp, scale=-1.0),
                      [tmpD[d].name], [Dm[d].name])
                    A(lambda e: e.activation(out=ibc[d][:], in_=Abc[d][:, c * 128:(c + 1) * 128], func=AF.Exp,
                                             bias=ubc[d][:, c:c + 1], scale=-1.0),
                      [Abc[d].name, ubc[d].name], [ibc[d].name])
                    V(lambda e: e.tensor_tensor(PT[d][:], ps_misc[:, o:o + 128], Dm[d][:], ALU.mult),
                      ["ps_misc", Dm[d].name], [PT[d].name])
                    V(lambda e: e.tensor_tensor(qs[d][:], qT[:], ibc[d][:, None, :].broadcast_to([128, 4, 128]),
                                                ALU.mult), [qT.name, ibc[d].name], [qs[d].name])
                    kw_, dI_ = kw[d][it % 2], decI[d][it % 2]
                    A(lambda e: e.activation(out=kw_[:], in_=kt[:], func=AF.Copy, scale=cols[d][:, 1, c:c + 1]),
                      [kt.name, cols[d].name], [kw_.name])
                    A(lambda e: e.activation(out=dI_[:], in_=identb[:], func=AF.Copy, scale=dbc[d][:, c:c + 1]),
                      ["identb", dbc[d].name], [dI_.name])

                def dc_round(d, it, rnd):
                    vt = vtc[d][it % 3]
                    kw_, dI_ = kw[d][it % 2], decI[d][it % 2]
                    CB = CBn(d)
                    for jj in range(2):
                        j = 2 * rnd + jj
                        pd = ps_dc[d][jj]
                        P.mm([lambda e, j=j, pd=pd: e.matmul(pd[:], kw_[:, j * 128:(j + 1) * 128], vt[:],
                                                             start=True, stop=False),
                              lambda e, j=j, pd=pd: e.matmul(pd[:], dI_[:], Cbf[d][:, j, :], start=False,
                                                             stop=True)],
                             reads=L(kw_.name, vt.name, dI_.name, CB[j]), writes=L(pd.name))

                def dc_evac(d, rnd):
                    CB = CBn(d)
                    for jj in range(2):
                        j = 2 * rnd + jj
                        pd = ps_dc[d][jj]
                        if jj == 0:
                            A(lambda e, j=j, pd=pd: e.activation(out=Cbf[d][:, j, :], in_=pd[:], func=AF.Copy),
                              [pd.name], [CB[j]])
                        else:
                            V(lambda e, j=j, pd=pd: e.tensor_copy(Cbf[d][:, j, :], pd[:]), [pd.name], [CB[j]])

                def stage2a(d, c, it):
                    sl = it % 3
                    vt = vtc[d][sl]
                    kw_ = kw[d][it % 2]
                    o = d * 256
                    CB = CBn(d)
                    P.mm([lambda e: e.matmul(ps_num[d][:], PT[d][:], vt[:], start=True, stop=False)] +
                         [(lambda e, j=j: e.matmul(ps_num[d][:], qs[d][:, j, :], Cbf[d][:, j, :], start=False,
                                                   stop=(j == 3))) for j in range(4)] +
                         [lambda e: e.matmul(ps_misc[:, o + 128:o + 129], PT[d][:], onesb[:, 0:1], start=True,
                                             stop=False)] +
                         [(lambda e, j=j: e.matmul(ps_misc[:, o + 128:o + 129], qs[d][:, j, :], nbf[d][:, j:j + 1],
                                                   start=False, stop=(j == 3))) for j in range(4)] +
                         [(lambda e, j=j: e.matmul(ps_misc[:, o + 136 + j:o + 137 + j],
                                                   kw_[:, j * 128:(j + 1) * 128],
                                                   onesb[:, 0:1], start=True, stop=True)) for j in range(4)],
                         reads=L(PT[d].name, vt.name, qs[d].name, nbf[d].name, "onesb", kw_.name, *CB),
                         writes=L(ps_num[d].name, "ps_misc"))
                    dc_round(d, it, 0)

                def stage3a(d, c, it):
                    sl = it % 3
                    o = d * 256
                    hb = hbuf[d][it % 3]
                    V(lambda e: e.tensor_copy(den[d][:, 0:1], ps_misc[:, o + 128:o + 129]), ["ps_misc"],
                      [den[d].name])
                    V(lambda e: e.tensor_scalar(den[d][:, 1:2], den[d][:, 0:1], -1.0, None, ALU.mult),
                      [den[d].name], [den[d].name])
                    V(lambda e: e.tensor_tensor(den[d][:, 0:1], den[d][:, 0:1], den[d][:, 1:2], ALU.max),
                      [den[d].name], [den[d].name])
                    V(lambda e: e.tensor_tensor(den[d][:, 0:1], den[d][:, 0:1], cols[d][:, 3, c:c + 1], ALU.max),
                      [den[d].name, cols[d].name], [den[d].name])
                    V(lambda e: e.reciprocal(rr[d][:], den[d][:, 0:1]), [den[d].name], [rr[d].name])
                    sg = sgc[d][sl]
                    V(lambda e: e.scalar_tensor_tensor(hb[:], ps_num[d][:], rr[d][:, 0:1], sg[:], ALU.mult,
                                                       ALU.mult), [ps_num[d].name, rr[d].name, sg.name], [hb.name])
                    V(lambda e: e.scalar_tensor_tensor(n32[d][:], n32[d][:], dbc[d][:, c:c + 1],
                                                       ps_misc[:, o + 136:o + 140], ALU.mult, ALU.add),
                      [n32[d].name, dbc[d].name, "ps_misc"], [n32[d].name])
                    V(lambda e: e.tensor_copy(nbf[d][:], n32[d][:]), [n32[d].name], [nbf[d].name])
                    dc_evac(d, 0)

                def stage2b(d, c, it):
                    dc_round(d, it, 1)

                def stage3b(d, c, it):
                    hb = hbuf[d][it % 3]
                    dc_evac(d, 1)
                    if it < 32:
                        P.dma(hst_d[d, c], hb[:], reads=L(hb.name), writes=L("hst_d%d" % d))

                def ep1(d, c, it):
                    hb, ho = hbuf[d][it % 3], hoth[d][it % 3]
                    V(lambda e: e.tensor_tensor(hs[d][:], hb[:], ho[:], ALU.add), [hb.name, ho.name], [hs[d].name])
                    A(lambda e: e.activation(out=junk[d][:], in_=hs[d][:], func=AF.Square, accum_out=ssq[d][:]),
                      [hs[d].name], [junk[d].name, ssq[d].name])

                def ep2(d, c, it):
                    A(lambda e: e.activation(out=ssq[d][:], in_=ssq[d][:], func=AF.Ln, bias=epsc[:, 0:1],
                                             scale=1.0 / 512.0), [ssq[d].name, "epsc"], [ssq[d].name])
                    A(lambda e: e.activation(out=ssq[d][:], in_=ssq[d][:], func=AF.Exp, scale=-0.5),
                      [ssq[d].name], [ssq[d].name])
                    A(lambda e: e.activation(out=hn[d][:], in_=hs[d][:], func=AF.Copy, scale=ssq[d][:, 0:1]),
                      [hs[d].name, ssq[d].name], [hn[d].name])

                def ep3(d, c, it):
                    trv = ps_tr[:, d * 512:(d + 1) * 512]
                    P.mm([(lambda e, j=j: e.transpose(trv[:, j * 128:(j + 1) * 128],
                                                      hn[d][:, j * 128:(j + 1) * 128], identb[:]))
                          for j in range(4)],
                         reads=L(hn[d].name, "identb"), writes=L("ps_tr"))

                def ep4(d, c, it):
                    trv = ps_tr[:, d * 512:(d + 1) * 512]
                    xcT_c, sz = xcc[d][it % 3], szc[d][it % 3]
                    V(lambda e: e.tensor_tensor(e2[d][:], trv.rearrange("p (j t) -> p j t", t=128), sz[:], ALU.mult),
                      ["ps_tr", sz.name], [e2[d].name])
                    uo = u_o[d][it % 2]
                    V(lambda e: e.tensor_tensor(uo[:], e2[d][:], xcT_c[:], ALU.add), [e2[d].name, xcT_c.name],
                      [uo.name])
                    P.dma(uin_d[c // 4, :, (c % 4) * 128:(c % 4 + 1) * 128].rearrange("(j p) t -> p j t", p=128),
                          uo[:], reads=L(uo.name), writes=L("uin%d" % (c // 4)))
                    if (d == 0 and c % 4 == 3) or (d == 1 and c % 4 == 0):
                        exchange(c // 4)

                NIT = 64
                load_step(0, 0, 0)
                load_step(1, 63, 0)
                load_step(0, 1, 1)
                load_step(1, 62, 1)
                for it in range(NIT):
                    late = (it + 1 == 32)
                    if it + 2 < NIT:
                        load_step(0, it + 2, it + 2)
                        load_step(1, 61 - it, it + 2)
                    if it + 1 < NIT and not late:
                        load_ep(0, it + 1, it + 1)
                        load_ep(1, 62 - it, it + 1)
                    cc_ = (it, 63 - it)
                    nc_ = (it + 1, 62 - it)
                    pc_ = (it - 1, 64 - it)
                    if it == 0:
                        for d in D2:
                            stage1(d, cc_[d], 0)
                    for stg_fn, ep_fn in ((stage2a, ep1), (stage3a, ep2), ("next1", ep3), (stage2b, ep4),
                                          (stage3b, None)):
                        for d in D2:
                            if stg_fn == "next1":
                                if it + 1 < NIT:
                                    stage1(d, nc_[d], it + 1)
                            else:
                                stg_fn(d, cc_[d], it)
                        if ep_fn is not None and it - 1 >= 32:
                            for d in D2:
                                ep_fn(d, pc_[d], it - 1)
                    if late:
                        load_ep(0, it + 1, it + 1)
                        load_ep(1, 62 - it, it + 1)
                for ep_fn in (ep1, ep2, ep3, ep4):
                    for d in D2:
                        ep_fn(d, (63, 0)[d], 63)
                P.barrier(coll=False)
                if dbg:
                    P.dma(dbg_uown, uin_d, reads=L(*["uin%d" % tg for tg in range(16)]), writes=L("dbg_uown"))
                    P.dma(dbg_h[0, 0:32], hst_d[0, 0:32], reads=L("hst_d0", "hst_d1"), writes=L("dbg_h"))
                    P.dma(dbg_h[1, 32:64], hst_d[1, 32:64], reads=L("hst_d0", "hst_d1"), writes=L("dbg_h"))
                    P.barrier()
        if stop_after <= 3:
            P.run()
            return nc

        with ExitStack() as ph3:
            NT = 256
            NB3 = 2048 // NT
            Wg = sbt(ph3, "Wg", [128, 8, 2048], BF16)
            Wfo = sbt(ph3, "Wfo", [128, 8, 1024], BF16)
            Wml = sbt(ph3, "Wml", [128, 16, 1024], BF16)
            Wout = sbt(ph3, "Wout", [128, 8, 1024], BF16)
            stg = [sbt(ph3, "stg%d" % i, [128, 2048], F32) for i in range(2)]
            xb3 = [sbt(ph3, "xb3_%d" % i, [128, 8, NT], BF16) for i in range(2)]
            sq3 = sbt(ph3, "sq3", [128, 8, NT], BF16)
            R3 = [sbt(ph3, "R3_%d" % i, [128, NT], F32) for i in range(2)]
            utb = [sbt(ph3, "ut%d" % i, [128, 24, NT], BF16) for i in range(2)]
            gT = sbt(ph3, "gT", [128, 16, NT], BF16)
            yT = sbt(ph3, "yT", [128, 8, NT], BF16)
            tf3 = [sbt(ph3, "tf3_%d" % i, [128, NT], F32) for i in range(2)]
            ta = sbt(ph3, "ta", [128, NT], F32)
            tb_ = sbt(ph3, "tb_", [128, NT], F32)
            xres = sbt(ph3, "xres", [128, D], F32)
            xo = sbt(ph3, "xo", [128, D], F32)
            yo = sbt(ph3, "yo", [128, D], F32)
            fnw = sbt(ph3, "fnw", [128, D], F32)
            junk3 = sbt(ph3, "junk3", [128, D], BF16)
            ssq3 = sbt(ph3, "ssq3", [128, 1], F32)
            qoffs = sbt(ph3, "qoffs", [1, 2], I32)
            qreg = ph3.enter_context(nc.sync.register("qreg"))
            qreg2 = ph3.enter_context(nc.sync.register("qreg2"))
            ps_g = [pst(ph3, "ps_g%d" % i, [128, 512]) for i in range(2)]
            ps_a = pst(ph3, "ps_a", [128, 512])
            ps_b = pst(ph3, "ps_b", [128, 512])
            ps_o = [pst(ph3, "ps_o%d" % i, [128, 512]) for i in range(2)]
            ps_s3 = pst(ph3, "ps_s3", [128, 512])

            P.dma(qoffs[:], qoff_d, writes=L("qoffs"))
            P._emit_waits("sync", P._deps(L("qoffs"), []))
            qv = {}

            def load_q(e):
                e.reg_load(qreg, qoffs[0:1, 0:1])
                qv["v"] = e.snap(qreg, min_val=0, max_val=12)
                e.reg_load(qreg2, qoffs[0:1, 1:2])
                qv["f"] = e.snap(qreg2, min_val=0, max_val=6)

            P.raw("sync", load_q)
            P.dma(fnw[:], fnw_d.partition_broadcast(128), writes=L("fnw"))

            si = [0]

            def wload(dst_ap, src_ap, shape3, scale_col=None, wname="Wg"):
                sg = stg[si[0] % 2]
                si[0] += 1
                n = shape3[0] * shape3[1]
                view = sg[:, 0:n].rearrange("p (a n) -> p a n", n=shape3[1]) if shape3[0] > 1 else sg[:, 0:n]
                P.dma(view, src_ap, writes=L(sg.name))
                if scale_col is not None:
                    V(lambda e: e.tensor_scalar(dst_ap, view, scale_col, None, ALU.mult), [sg.name, "nw"],
                      [wname])
                elif si[0] % 2 == 0:
                    A(lambda e: e.activation(out=dst_ap, in_=view, func=AF.Copy), [sg.name], [wname])
                else:
                    V(lambda e: e.tensor_copy(dst_ap, view), [sg.name], [wname])

            for k in range(8):
                wload(Wg[:, k, :], wg_d[k * 128:(k + 1) * 128, :], (1, 2048), scale_col=nw[:, k:k + 1])
            wfo_v = wfo_d.rearrange("(a p) n -> p a n", p=128)
            wml_v = wml_d.rearrange("(a p) n -> p a n", p=128)
            wout_v = wout_d.rearrange("(a p) n -> p a n", p=128)
            for i in range(4):
                wload(Wfo[:, 2 * i:2 * i + 2, :], wfo_v[:, 2 * i:2 * i + 2, :], (2, 1024), wname="Wfo")
            for i in range(8):
                wload(Wml[:, 2 * i:2 * i + 2, :], wml_v[:, 2 * i:2 * i + 2, :], (2, 1024), wname="Wml")
            for i in range(4):
                wload(Wout[:, 2 * i:2 * i + 2, :], wout_v[:, 2 * i:2 * i + 2, :], (2, 1024), wname="Wout")

            uout_v = uout_d.rearrange("tg (a p) t -> p tg a t", p=128)
            ufout_v = ufout_d.rearrange("tg (a p) t -> p tg a t", p=128)
            g_i = [0]
            o_i = [0]
            def load3(tb):
                xs3 = stg[tb % 2]
                xsv = xs3[:].rearrange("p (k t) -> p k t", t=NT)
                P.dma(xsv, xTq_d.rearrange("(k p) t -> p k t", p=128)[:, :, tb * NT:(tb + 1) * NT],
                      writes=L(xs3.name))
                ut_ = utb[tb % 2]
                P.dma(None, None, reads=L(*["ufout%d" % tg for tg in range(8)]), writes=L(ut_.name),
                      fn=lambda e, tb=tb, ut_=ut_: e.dma_start(
                          out=ut_[:, 0:8, :].unsqueeze(1),
                          in_=ufout_v[:, bass.ds(qv["f"] + tb // 4, 1), :, (tb % 4) * NT:(tb % 4 + 1) * NT]))
                P.dma(None, None, reads=L(*["uout%d" % tg for tg in range(16)]), writes=L(ut_.name),
                      fn=lambda e, tb=tb, ut_=ut_: e.dma_start(
                          out=ut_[:, 8:24, :].unsqueeze(1),
                          in_=uout_v[:, bass.ds(qv["v"] + tb // 2, 1), :, (tb % 2) * NT:(tb % 2 + 1) * NT]))

            load3(0)
            for tb in range(NB3):
                X = xb3[tb % 2]
                R = R3[tb % 2]
                xs3 = stg[tb % 2]
                ut = utb[tb % 2]
                xsv = xs3[:].rearrange("p (k t) -> p k t", t=NT)
                if tb + 1 < NB3:
                    load3(tb + 1)
                A(lambda e, X=X, xsv=xsv: e.activation(out=X[:], in_=xsv, func=AF.Copy), [xs3.name], [X.name])
                G(lambda e, xsv=xsv: e.tensor_tensor(sq3[:], xsv, xsv, ALU.mult), [xs3.name], ["sq3"])
                P.mm([(lambda e, k=k: e.matmul(ps_s3[:, 0:NT], onesb[:], sq3[:, k, :], start=(k == 0), stop=(k == 7)))
                      for k in range(8)], reads=L("onesb", "sq3"), writes=L("ps_s3"))
                A(lambda e, R=R: e.activation(out=R[:], in_=ps_s3[:, 0:NT], func=AF.Sqrt, bias=EPS, scale=1.0 / D),
                  ["ps_s3"], [R.name])
                V(lambda e, R=R: e.reciprocal(R[:], R[:]), [R.name], [R.name])
                for gc in range(16):
                    pp = ps_g[g_i[0] % 2]
                    tf = tf3[g_i[0] % 2]
                    g_i[0] += 1
                    P.mm([(lambda e, k=k, gc=gc, pp=pp, X=X: e.matmul(pp[:, 0:NT], Wg[:, k, gc * 128:(gc + 1) * 128],
                                                                       X[:, k, :], start=(k == 0), stop=(k == 7)))
                          for k in range(8)], reads=L("Wg", X.name), writes=L(pp.name))
                    V(lambda e, pp=pp, tf=tf, R=R: e.tensor_tensor(tf[:], pp[:, 0:NT], R[:], ALU.mult),
                      [pp.name, R.name], [tf.name])
                    A(lambda e, gc=gc, tf=tf: e.activation(out=gT[:, gc, :], in_=tf[:], func=AF.Sigmoid),
                      [tf.name], ["gT"])
                for m in range(8):
                    P.mm([(lambda e, a=a, m=m, ut=ut: e.matmul(ps_a[:, 0:NT], Wfo[:, a, m * 128:(m + 1) * 128],
                                                        ut[:, a, :], start=(a == 0),
                                                        stop=(a == 7))) for a in range(8)],
                         reads=L("Wfo", ut.name), writes=L("ps_a"))
                    P.mm([(lambda e, a=a, m=m, ut=ut: e.matmul(ps_b[:, 0:NT], Wml[:, a, m * 128:(m + 1) * 128],
                                                        ut[:, 8 + a, :], start=(a == 0),
                                                        stop=(a == 15))) for a in range(16)],
                         reads=L("Wml", ut.name), writes=L("ps_b"))
                    V(lambda e, m=m: e.tensor_tensor(ta[:], ps_a[:, 0:NT], gT[:, m, :], ALU.mult),
                      ["ps_a", "gT"], ["ta"])
                    V(lambda e, m=m: e.tensor_tensor(tb_[:], ps_b[:, 0:NT], gT[:, 8 + m, :], ALU.mult),
                      ["ps_b", "gT"], ["tb_"])
                    G(lambda e, m=m: e.tensor_tensor(yT[:, m, :], ta[:], tb_[:], ALU.add), ["ta", "tb_"], ["yT"])
                for tt in range(NT // 128):
                    row0 = (tb * (NT // 128) + tt) * 128
                    P.dma(xres[:], xq_d[row0:row0 + 128, :], writes=L("xres"))
                    for half in range(2):
                        pp = ps_o[o_i[0] % 2]
                        o_i[0] += 1
                        P.mm([(lambda e, m=m, tt=tt, half=half, pp=pp: e.matmul(
                            pp[:], yT[:, m, tt * 128:(tt + 1) * 128], Wout[:, m, half * 512:(half + 1) * 512],
                            start=(m == 0), stop=(m == 7))) for m in range(8)],
                            reads=L("Wout", "yT"), writes=L(pp.name))
                        V(lambda e, half=half, pp=pp: e.tensor_tensor(xo[:, half * 512:(half + 1) * 512], pp[:],
                                                                      xres[:, half * 512:(half + 1) * 512], ALU.add),
                          [pp.name, "xres"], ["xo"])
                    A(lambda e: e.activation(out=junk3[:], in_=xo[:], func=AF.Square, accum_out=ssq3[:]),
                      ["xo"], ["junk3", "ssq3"])
                    A(lambda e: e.activation(out=ssq3[:], in_=ssq3[:], func=AF.Sqrt, bias=EPS, scale=1.0 / D),
                      ["ssq3"], ["ssq3"])
                    V(lambda e: e.reciprocal(ssq3[:], ssq3[:]), ["ssq3"], ["ssq3"])
                    V(lambda e: e.scalar_tensor_tensor(yo[:], xo[:], ssq3[:, 0:1], fnw[:], ALU.mult, ALU.mult),
                      ["xo", "ssq3", "fnw"], ["yo"])
                    P.dma(out_d[row0:row0 + 128, :], yo[:], reads=L("yo"), writes=L("out_d"))
            P.barrier()
        P.run()
    return nc


def _consts():
    c = {}
    c["identb"] = np.eye(128, dtype=np.float32).astype(ml_dtypes.bfloat16)
    c["identf"] = np.eye(128, dtype=np.float32)
    s = np.arange(128)[:, None]
    t = np.arange(128)[None, :]
    masks = np.zeros((128, 2, 128), np.float32)
    masks[:, 0, :] = np.where(s <= t, 0.0, 1.0e4)
    masks[:, 1, :] = np.where(s >= t, 0.0, 1.0e4)
    c["masks"] = masks
    cp = np.arange(64)[:, None]
    cc = np.arange(64)[None, :]
    ustr = np.zeros((64, 2, 64), np.float32)
    ustr[:, 0, :] = (cp < cc)
    ustr[:, 1, :] = (cp > cc)
    c["ustr"] = ustr
    scale = 1.0 / np.sqrt(8192.0 * 256.0)
    n = np.arange(256, dtype=np.float64)
    ang = 2.0 * np.pi * np.outer(n, n) / 256.0
    dftc = np.concatenate([np.cos(ang), -np.sin(ang)], axis=1) * scale
    c["dftc"] = np.ascontiguousarray(dftc.reshape(2, 128, 512).transpose(1, 0, 2)).astype(np.float32).astype(
        ml_dtypes.bfloat16)
    n1 = np.arange(128, dtype=np.float64)[:, None]
    k1 = np.arange(128, dtype=np.float64)[None, :]
    m1 = np.zeros((64, 128, 384), np.float64)
    for n2 in range(64):
        th = 2.0 * np.pi * k1 * (64.0 * n1 + n2) / 8192.0
        m1[n2, :, 0:128] = np.cos(th)
        m1[n2, :, 128:256] = np.sin(th)
        m1[n2, :, 256:384] = -np.sin(th)
    c["m1"] = m1.astype(np.float32).astype(ml_dtypes.bfloat16)
    n2 = np.arange(64, dtype=np.float64)[:, None]
    k2 = np.arange(64, dtype=np.float64)[None, :]
    ph = 2.0 * np.pi * n2 * k2 / 64.0
    c["cs2"] = np.concatenate([np.cos(ph), np.sin(ph)], axis=0).astype(np.float32).astype(ml_dtypes.bfloat16)
    return c


def prep_inputs(inp):
    f32 = np.float32
    x = np.asarray(inp["x"], f32)
    w_in = np.asarray(inp["w_in"], f32)[0]
    consts = _consts()
    xT = [np.ascontiguousarray(x[b].T) for b in range(2)]
    maps = []
    for core in range(8):
        b, g = core // 4, core % 4
        m = dict(consts)
        m["xT"] = xT[b]
        m["xTq"] = np.ascontiguousarray(xT[b][:, 2048 * g:2048 * (g + 1)])
        m["xq"] = np.ascontiguousarray(x[b, 2048 * g:2048 * (g + 1), :])
        cols = np.concatenate([np.arange(1024 + 256 * g, 1024 + 256 * (g + 1)),
                               np.arange(2048 + 512 * g, 2048 + 512 * (g + 1)),
                               np.arange(4096 + 512 * g, 4096 + 512 * (g + 1)),
                               np.arange(6144 + 512 * g, 6144 + 512 * (g + 1))])
        m["w_own"] = np.ascontiguousarray(w_in[:, cols])
        m["wfT"] = np.ascontiguousarray(w_in[:, 256 * g:256 * (g + 1)].T)
        m["wg"] = np.ascontiguousarray(w_in[:, 8192:10240])
        m["nw"] = np.ascontiguousarray(np.asarray(inp["norm_w"], f32)[0].reshape(8, 128).T)
        cw = np.asarray(inp["conv_w"], f32)[0][:, 512 * g:512 * (g + 1)]
        m["convw"] = np.ascontiguousarray(cw.reshape(5, 4, 128).transpose(2, 1, 0))
        m["convb"] = np.ascontiguousarray(np.asarray(inp["conv_b"], f32)[0][512 * g:512 * (g + 1)].reshape(4, 128).T)
        bd = np.zeros((128, 3, 4, 128), f32)
        bdT = np.zeros((128, 3, 4, 128), f32)
        for s_i, nm in enumerate(["w_q", "w_k", "w_v"]):
            w = np.asarray(inp[nm], f32)[0]
            for j in range(4):
                for bl in range(32):
                    blk = w[128 * g + 32 * j + bl]
                    bd[4 * bl:4 * bl + 4, s_i, j, 4 * bl:4 * bl + 4] = blk
                    bdT[4 * bl:4 * bl + 4, s_i, j, 4 * bl:4 * bl + 4] = blk.T
        m["bd"] = bd
        m["bdT"] = bdT
        gnames = ["w_igate_fwd", "w_fgate_fwd", "w_igate_bwd", "w_fgate_bwd"]
        wgate = np.zeros((128, 12, 16), f32)
        for t_i, nm in enumerate(gnames):
            w = np.asarray(inp[nm], f32)[0]
            for s_i in range(3):
                rows = w[2048 * s_i + 512 * g:2048 * s_i + 512 * (g + 1), :]
                wgate[:, 4 * s_i:4 * s_i + 4, 4 * t_i:4 * t_i + 4] = rows.reshape(4, 128, 4).transpose(1, 0, 2)
        m["wgate"] = wgate
        sel = np.zeros((64, 4), f32)
        sel[:, g] = 1.0
        m["sel"] = sel
        bnames = ["b_igate_fwd", "b_fgate_fwd", "b_igate_bwd", "b_fgate_bwd"]
        gb = np.zeros((64, 4), f32)
        for t_i, nm in enumerate(bnames):
            gb[:, t_i] = np.asarray(inp[nm], f32)[0][g]
        m["gb"] = gb
        m["hnw"] = np.ascontiguousarray(np.asarray(inp["hnorm_w"], f32)[0][512 * g:512 * (g + 1)].reshape(4, 128).T)
        m["skw"] = np.ascontiguousarray(np.asarray(inp["skip_w"], f32)[0][512 * g:512 * (g + 1)].reshape(4, 128).T)
        m["wfo"] = np.asarray(inp["w_fourier"], f32)[0]
        m["wml"] = np.asarray(inp["w_mlstm"], f32)[0]
        m["wout"] = np.asarray(inp["w_out"], f32)[0]
        m["fnw"] = np.asarray(inp["final_norm_w"], f32).reshape(1, 1024)
        m["qoff"] = np.array([[4 * g, 2 * g]], dtype=np.int32)
        maps.append({"i_" + k: v for k, v in m.items()})
    return maps


_NC_CACHE = {}


def kernel(**inputs):
    if "nc" not in _NC_CACHE:
        _NC_CACHE["nc"] = build_program()
    nc = _NC_CACHE["nc"]
    maps = prep_inputs(inputs)
    res = run_bass_kernel_spmd(nc, maps, core_ids=list(range(8)))
    out = np.zeros((2, S, D), np.float32)
    for core in range(8):
        b, g = core // 4, core % 4
        out[b, 2048 * g:2048 * (g + 1), :] = np.asarray(res.results[core]["out"], np.float32)
    return out
```

```python
import numpy as np
import ml_dtypes
from contextlib import ExitStack

import concourse.bass as bass
import concourse.mybir as mybir
from concourse.bass_utils import run_bass_kernel_spmd

F32 = mybir.dt.float32
BF16 = mybir.dt.bfloat16
I32 = mybir.dt.int32
ALU = mybir.AluOpType
AF = mybir.ActivationFunctionType
AX = mybir.AxisListType

ENGS = ["tensor", "vector", "scalar", "gpsimd", "sync"]
S = 8192
D = 1024
NCH = 64
EPS = 1e-6


class Buf:
    __slots__ = ("name", "lw", "rd")

    def __init__(self, name):
        self.name = name
        self.lw = None
        self.rd = {}


class Prog:
    def __init__(self, nc, stack, n_dma_sems=24):
        self.nc = nc
        self.q = {e: [] for e in ENGS}
        self.cnt = {e: 0 for e in ENGS}
        self.waited = {e: {} for e in ENGS}
        self.semobj = {}
        for e in ENGS:
            self.semobj[("e", e)] = stack.enter_context(nc.semaphore("se_" + e))
        self.dsem_keys = []
        self.dval = {}
        for i in range(n_dma_sems):
            k = ("d", i)
            self.semobj[k] = stack.enter_context(nc.semaphore("sd_%d" % i))
            self.dsem_keys.append(k)
            self.dval[k] = 0
        self.ck = ("c", 0)
        self.semobj[self.ck] = stack.enter_context(nc.semaphore("s_cc"))
        self.cval = 0
        self.dnext = 0

    def _deps(self, reads, writes, skip_key=None):
        deps = {}

        def add(k, v):
            if k == skip_key:
                return
            if deps.get(k, 0) < v:
                deps[k] = v

        for b in reads:
            if b.lw is not None:
                add(*b.lw)
        for b in writes:
            if b.lw is not None:
                add(*b.lw)
            for k, v in b.rd.items():
                add(k, v)
        return deps

    def _emit_waits(self, eng, deps):
        w = self.waited[eng]
        for k, v in deps.items():
            if w.get(k, 0) < v:
                w[k] = v
                sem = self.semobj[k]
                self.q[eng].append(lambda e, sem=sem, v=v: e.wait_ge(sem, v))

    def _mark(self, reads, writes, k, v):
        for b in reads:
            if b.rd.get(k, 0) < v:
                b.rd[k] = v
        for b in writes:
            b.lw = (k, v)
            b.rd = {}

    def op(self, eng, fn, reads=(), writes=()):
        k = ("e", eng)
        deps = self._deps(reads, writes, skip_key=k if eng == "tensor" else None)
        self._emit_waits(eng, deps)
        self.cnt[eng] += 1
        v = self.cnt[eng]
        sem = self.semobj[k]
        self.q[eng].append(lambda e, fn=fn, sem=sem: fn(e).then_inc(sem, 1))
        self._mark(reads, writes, k, v)

    def mm(self, fns, reads=(), writes=()):
        k = ("e", "tensor")
        deps = self._deps(reads, writes, skip_key=k)
        self._emit_waits("tensor", deps)
        self.cnt["tensor"] += 1
        v = self.cnt["tensor"]
        sem = self.semobj[k]
        for fn in fns[:-1]:
            self.q["tensor"].append(lambda e, fn=fn: fn(e))
        self.q["tensor"].append(lambda e, fn=fns[-1], sem=sem: fn(e).then_inc(sem, 1))
        self._mark(reads, writes, k, v)

    def dma(self, out, in_, reads=(), writes=(), eng="sync", fn=None, **kw):
        k = self.dsem_keys[self.dnext % len(self.dsem_keys)]
        self.dnext += 1
        deps = self._deps(reads, writes)
        if self.dval[k] > 0 and deps.get(k, 0) < self.dval[k]:
            deps[k] = self.dval[k]
        self._emit_waits(eng, deps)
        self.dval[k] += 16
        v = self.dval[k]
        sem = self.semobj[k]
        if fn is None:
            self.q[eng].append(
                lambda e, out=out, in_=in_, sem=sem, kw=kw: e.dma_start(out=out, in_=in_, **kw).then_inc(sem, 16))
        else:
            self.q[eng].append(lambda e, fn=fn, sem=sem: fn(e).then_inc(sem, 16))
        self._mark(reads, writes, k, v)

    def coll(self, fn, reads=(), writes=()):
        deps = self._deps(reads, writes)
        self._emit_waits("gpsimd", deps)
        self.cval += 1
        sem = self.semobj[self.ck]
        self.q["gpsimd"].append(lambda e, fn=fn, sem=sem: fn(e).then_inc(sem, 1))
        self._mark(reads, writes, self.ck, self.cval)

    def raw(self, eng, fn):
        self.q[eng].append(fn)

    def barrier(self, coll=True):
        deps = {}
        for e in ENGS:
            if self.cnt[e] > 0:
                deps[("e", e)] = self.cnt[e]
        for k in self.dsem_keys:
            if self.dval[k] > 0:
                deps[k] = self.dval[k]
        if self.cval > 0 and coll:
            deps[self.ck] = self.cval
        for e in ENGS:
            self._emit_waits(e, dict(deps))

    def run(self):
        nc = self.nc
        q = self.q
        with nc.Block() as block:
            @block.tensor
            def _(e):
                for f in q["tensor"]:
                    f(e)

            @block.vector
            def _(e):
                for f in q["vector"]:
                    f(e)

            @block.scalar
            def _(e):
                for f in q["scalar"]:
                    f(e)

            @block.gpsimd
            def _(e):
                for f in q["gpsimd"]:
                    f(e)

            @block.sync
            def _(e):
                for f in q["sync"]:
                    f(e)


GROUPS = [[0, 1, 2, 3], [4, 5, 6, 7]]


def build_program(dbg=False, stop_after=99, skip1=False, nocoll=False, parts="abg", tiny=False):
    nc = bass.Bass("TRN2", target_bir_lowering=False)

    def din(name, shape, dt=F32):
        if tiny and name in (("xT", "xTq", "xq", "wg", "wfo", "wml", "wout") if stop_after <= 3 else ("xT",)):
            shape = [2, 16]
        return nc.dram_tensor("i_" + name, shape, dt, kind="ExternalInput").ap()

    def dscr(name, shape, dt=BF16, ext=False):
        if ext and dbg:
            return nc.dram_tensor(name, shape, dt, kind="ExternalOutput").ap()
        return nc.dram_tensor(name, shape, dt).ap()

    xT_d = din("xT", [D, S])
    xTq_d = din("xTq", [D, 2048])
    xq_d = din("xq", [2048, D])
    wown_d = din("w_own", [D, 1792])
    wfT_d = din("wfT", [256, D])
    wg_d = din("wg", [D, 2048])
    nw_d = din("nw", [128, 8])
    dftc_d = din("dftc", [128, 2, 512], BF16)
    convw_d = din("convw", [128, 4, 5])
    convb_d = din("convb", [128, 4])
    bd_d = din("bd", [128, 3, 4, 128])
    bdT_d = din("bdT", [128, 3, 4, 128])
    wgate_d = din("wgate", [128, 12, 16])
    sel_d = din("sel", [64, 4])
    gb_d = din("gb", [64, 4])
    hnw_d = din("hnw", [128, 4])
    skw_d = din("skw", [128, 4])
    wfo_d = din("wfo", [D, D])
    wml_d = din("wml", [2048, D])
    wout_d = din("wout", [D, D])
    fnw_d = din("fnw", [1, D])
    qoff_d = din("qoff", [1, 2], I32)
    m1_d = din("m1", [64, 128, 384], BF16)
    cs2_d = din("cs2", [128, 64], BF16)
    identb_d = din("identb", [128, 128], BF16)
    identf_d = din("identf", [128, 128])
    mask_d = din("masks", [128, 2, 128])
    ustr_d = din("ustr", [64, 2, 64])
    out_d = nc.dram_tensor("out", [2048, D], F32, kind="ExternalOutput").ap()

    Zs_d = dscr("Zs", [S, 512], BF16, ext=True)
    A1_d = dscr("A1s", [64, 128, 512], BF16)
    szfT_d = dscr("szfT", [256, S], BF16, ext=True)
    szmT_d = dscr("szmT", [NCH, 128, 4, 128], BF16, ext=True)
    xcT_d = dscr("xcT", [NCH, 128, 4, 128], BF16, ext=True)
    qT_d = dscr("qT", [NCH, 128, 4, 128], BF16, ext=True)
    kT_d = dscr("kT", [NCH, 128, 4, 128], BF16, ext=True)
    ktm_d = dscr("ktm", [S, 512], BF16, ext=True)
    vtm_d = dscr("vtm", [S, 512], BF16, ext=True)
    sigo_d = dscr("sigo", [S, 512], BF16, ext=True)
    gp_d = dscr("gp", [16, S], F32)
    gr_d = dscr("gr", [16, S], F32)
    arow_d = dscr("arow", [2, S], F32)
    hst_d = dscr("hst", [2, NCH, 128, 512], BF16)
    uin_d = dscr("uin", [16, 512, 512], BF16)
    uout_d = dscr("uout", [16, 2048, 512], BF16)
    ufin_d = dscr("ufin", [8, 256, 1024], BF16)
    ufout_d = dscr("ufout", [8, 1024, 1024], BF16)
    if dbg:
        dbg_gr = nc.dram_tensor("dbg_gr", [16, S], F32, kind="ExternalOutput").ap()
        dbg_cols = nc.dram_tensor("dbg_cols", [2, 128, 5, 64], F32, kind="ExternalOutput").ap()
        dbg_uown = nc.dram_tensor("dbg_uown", [16, 512, 512], BF16, kind="ExternalOutput").ap()
        dbg_h = nc.dram_tensor("dbg_h", [2, NCH, 128, 512], BF16, kind="ExternalOutput").ap()

    B = {}

    def bf(name):
        if name not in B:
            B[name] = Buf(name)
        return B[name]

    with ExitStack() as top:
        P = Prog(nc, top)

        def sbt(st, name, shape, dt=F32):
            return st.enter_context(nc.sbuf_tensor(name, shape, dt))

        def pst(st, name, shape, dt=F32):
            return st.enter_context(nc.psum_tensor(name, shape, dt))

        identb = sbt(top, "identb", [128, 128], BF16)
        identf = sbt(top, "identf", [128, 128], F32)
        onesb = sbt(top, "onesb", [128, 128], BF16)
        onesf = sbt(top, "onesf", [128, 128], F32)
        nw = sbt(top, "nw", [128, 8], F32)
        P.dma(identb[:], identb_d, writes=[bf("identb")])
        P.dma(identf[:], identf_d, writes=[bf("identf")])
        P.dma(nw[:], nw_d, writes=[bf("nw")])
        hnw = sbt(top, "hnw", [128, 4], F32)
        skw = sbt(top, "skw", [128, 4], F32)
        P.dma(hnw[:], hnw_d, writes=[bf("hnw")])
        P.dma(skw[:], skw_d, writes=[bf("skw")])
        P.op("vector", lambda e: e.memset(onesb[:], 1.0), writes=[bf("onesb")])
        P.op("vector", lambda e: e.memset(onesf[:], 1.0), writes=[bf("onesf")])
        epsc = sbt(top, "epsc", [128, 1], F32)
        P.op("vector", lambda e: e.memset(epsc[:], EPS), writes=[bf("epsc")])

        def xload(src_d, t0, xs, tag):
            src = src_d.rearrange("(k p) t -> p k t", p=128)[:, :, t0:t0 + 512]
            P.dma(xs[:], src, writes=[bf("xs" + tag)])

        def xblock(src_d, t0, xs, xb, sq, Rbc, Rcol, ps_st, ps_sc, tag, want_col, after_cast=None):
            P.op("scalar", lambda e: e.activation(out=xb[:], in_=xs[:], func=AF.Copy),
                 reads=[bf("xs" + tag)], writes=[bf(xb.name)])
            P.op("gpsimd", lambda e: e.tensor_tensor(sq[:], xs[:], xs[:], ALU.mult),
                 reads=[bf("xs" + tag)], writes=[bf("sq" + tag)])
            if after_cast is not None:
                after_cast()
            P.mm([(lambda e, k=k: e.matmul(ps_st[:], onesb[:], sq[:, k, :], start=(k == 0), stop=(k == 7)))
                  for k in range(8)],
                 reads=[bf("onesb"), bf("sq" + tag)], writes=[bf(ps_st.name)])
            P.op("scalar", lambda e: e.activation(out=Rbc[:], in_=ps_st[:], func=AF.Sqrt, bias=EPS, scale=1.0 / D),
                 reads=[bf(ps_st.name)], writes=[bf(Rbc.name)])
            P.op("vector", lambda e: e.reciprocal(Rbc[:], Rbc[:]), reads=[bf(Rbc.name)], writes=[bf(Rbc.name)])
            if want_col:
                P.mm([(lambda e, tt=tt: e.matmul(ps_sc[:, tt:tt + 1], Rbc[0:1, tt * 128:(tt + 1) * 128],
                                                  onesf[0:1, 0:1], start=True, stop=True)) for tt in range(4)],
                     reads=[bf(Rbc.name), bf("onesf")], writes=[bf(ps_sc.name)])
                P.op("vector", lambda e: e.tensor_copy(Rcol[:], ps_sc[:, 0:4]),
                     reads=[bf(ps_sc.name)], writes=[bf(Rcol.name)])

        with ExitStack() as ph1:
            Wb = sbt(ph1, "Wb", [128, 8, 1792], BF16)
            Wfz = sbt(ph1, "Wfz", [128, 8, 512], BF16)
            wstage = sbt(ph1, "wstage", [128, 1792], F32)
            bdb = sbt(ph1, "bdb", [128, 3, 4, 128], BF16)
            diagW = sbt(ph1, "diagW", [128, 4, 5, 128], BF16)
            Wgc = sbt(ph1, "Wgc", [128, 4, 16], BF16)
            Wgm = sbt(ph1, "Wgm", [128, 4, 16], BF16)
            convw = sbt(ph1, "convw", [128, 4, 5], F32)
            convb = sbt(ph1, "convb", [128, 4], F32)
            xs = sbt(ph1, "xs", [128, 8, 512], F32)
            xb = [sbt(ph1, "xb%d" % i, [128, 8, 512], BF16) for i in range(2)]
            sq = sbt(ph1, "sq", [128, 8, 512], BF16)
            Rbc = [sbt(ph1, "Rbc%d" % i, [128, 512], F32) for i in range(2)]
            Rcol = [sbt(ph1, "Rcol%d" % i, [128, 4], F32) for i in range(2)]
            tmpf = [sbt(ph1, "tmpf%d" % i, [128, 512], F32) for i in range(2)]
            szf_o = sbt(ph1, "szf_o", [128, 2, 512], BF16)
            szm_o = [sbt(ph1, "szm_o%d" % i, [128, 4, 4, 128], BF16) for i in range(2)]
            w1_o = sbt(ph1, "w1_o", [128, 4, 4, 128], BF16)
            xmw = [sbt(ph1, "xmw%d" % i, [128, 4, 516], BF16) for i in range(3)]
            z_o = sbt(ph1, "z_o", [128, 4, 512], BF16)
            so_o = sbt(ph1, "so_o", [128, 4, 512], BF16)
            xc = sbt(ph1, "xc", [128, 4, 512], BF16)
            xc_o = sbt(ph1, "xc_o", [128, 4, 4, 128], BF16)
            q_o = sbt(ph1, "q_o", [128, 4, 4, 128], BF16)
            k_o = sbt(ph1, "k_o", [128, 4, 4, 128], BF16)
            ktm_o = sbt(ph1, "ktm_o", [128, 4, 512], BF16)
            vtm_o = sbt(ph1, "vtm_o", [128, 4, 512], BF16)
            gp_o = sbt(ph1, "gp_o", [16, 512], F32)

            ps_pj = [pst(ph1, "ps_pj%d" % i, [128, 512]) for i in range(2)]
            ps_tm = [pst(ph1, "ps_tm%d" % i, [128, 512]) for i in range(2)]
            ps_st = pst(ph1, "ps_st", [128, 512])
            ps_cv = [pst(ph1, "ps_cv%d" % i, [128, 512]) for i in range(2)]
            ps_sm = pst(ph1, "ps_sm", [128, 512])

            with ExitStack() as ph0:
                wfTs = sbt(ph0, "wfTs", [128, 2, D], F32)
                wfTb = sbt(ph0, "wfTb", [128, 2, D], BF16)
                dftc = sbt(ph0, "dftc", [128, 2, 512], BF16)
                bds = sbt(ph0, "bds", [128, 3, 4, 128], F32)
                bdTs = sbt(ph0, "bdTs", [128, 3, 4, 128], F32)
                wgate = sbt(ph0, "wgate", [128, 12, 16], F32)

                for kc in range(8):
                    P.dma(wstage[:], wown_d[kc * 128:(kc + 1) * 128, :], writes=[bf("wstage")])
                    P.op("vector", lambda e, kc=kc: e.tensor_scalar(Wb[:, kc, :], wstage[:], nw[:, kc:kc + 1], None,
                                                                     ALU.mult),
                         reads=[bf("wstage"), bf("nw")], writes=[bf("Wb")])
                P.dma(wfTs[:], wfT_d.rearrange("(j p) d -> p j d", p=128), writes=[bf("wfTs")])
                P.dma(dftc[:], dftc_d, writes=[bf("dftc")])
                P.op("scalar", lambda e: e.activation(out=wfTb[:], in_=wfTs[:], func=AF.Copy),
                     reads=[bf("wfTs")], writes=[bf("wfTb")])
                for dc in range(8):
                    pp = ps_pj[dc % 2]
                    P.mm([(lambda e, j=j, dc=dc, pp=pp: e.matmul(pp[:], wfTb[:, j, dc * 128:(dc + 1) * 128],
                                                                  dftc[:, j, :], start=(j == 0), stop=(j == 1)))
                          for j in range(2)],
                         reads=[bf("wfTb"), bf("dftc")], writes=[bf(pp.name)])
                    P.op("vector", lambda e, dc=dc, pp=pp: e.tensor_scalar(Wfz[:, dc, :], pp[:], nw[:, dc:dc + 1],
                                                                            None, ALU.mult),
                         reads=[bf(pp.name), bf("nw")], writes=[bf("Wfz")])
                P.dma(bds[:], bd_d, writes=[bf("bds")])
                P.dma(bdTs[:], bdT_d, writes=[bf("bdTs")])
                P.dma(wgate[:], wgate_d, writes=[bf("wgate")])
                P.dma(convw[:], convw_d, writes=[bf("convw")])
                P.dma(convb[:], convb_d, writes=[bf("convb")])
                P.op("scalar", lambda e: e.activation(out=bdb[:], in_=bds[:], func=AF.Copy),
                     reads=[bf("bds")], writes=[bf("bdb")])
                for j in range(4):
                    for tap in range(5):
                        P.op("vector", lambda e, j=j, tap=tap: e.tensor_scalar(
                            diagW[:, j, tap, :], identf[:], convw[:, j, tap:tap + 1], None, ALU.mult),
                            reads=[bf("identf"), bf("convw")], writes=[bf("diagW")])
                for j in range(4):
                    P.mm([lambda e, j=j: e.matmul(ps_sm[:, 0:16], bdTs[:, 0, j, :], wgate[:, j, :], start=True,
                                                   stop=False),
                          lambda e, j=j: e.matmul(ps_sm[:, 0:16], bdTs[:, 1, j, :], wgate[:, 4 + j, :], start=False,
                                                   stop=True),
                          lambda e, j=j: e.matmul(ps_sm[:, 16:32], bdTs[:, 2, j, :], wgate[:, 8 + j, :], start=True,
                                                   stop=True)],
                         reads=[bf("bdTs"), bf("wgate")], writes=[bf("ps_sm")])
                    P.op("vector", lambda e, j=j: e.tensor_copy(Wgc[:, j, :], ps_sm[:, 0:16]),
                         reads=[bf("ps_sm")], writes=[bf("Wgc")])
                    P.op("vector", lambda e, j=j: e.tensor_copy(Wgm[:, j, :], ps_sm[:, 16:32]),
                         reads=[bf("ps_sm")], writes=[bf("Wgm")])
                P.barrier()

            NB = 16
            NBR = 0 if skip1 else NB
            pj_i = [0]
            tm_i = [0]
            cv_i = [0]
            tf_i = [0]

            def conv_block(tb):
                win = xmw[tb % 3]
                wname = win.name
                for j in range(4):
                    pp = ps_cv[cv_i[0] % 2]
                    cv_i[0] += 1
                    P.mm([(lambda e, j=j, tap=tap, pp=pp: e.matmul(pp[:], diagW[:, j, tap, :],
                                                                    win[:, j, tap:tap + 512],
                                                                    start=(tap == 0), stop=(tap == 4)))
                          for tap in range(5)],
                         reads=[bf("diagW"), bf(wname)], writes=[bf(pp.name)])
                    P.op("scalar", lambda e, j=j, pp=pp: e.activation(out=xc[:, j, :], in_=pp[:], func=AF.Silu,
                                                                      bias=convb[:, j:j + 1], scale=1.0),
                         reads=[bf(pp.name), bf("convb")], writes=[bf("xc")])
                P.op("gpsimd", lambda e: e.tensor_tensor(xc_o[:].rearrange("p c j t -> p j c t"),
                                                         xc[:].rearrange("p j (c t) -> p j c t", t=128),
                                                         skw[:, :, None, None].broadcast_to([128, 4, 4, 128]),
                                                         ALU.mult),
                     reads=[bf("xc"), bf("skw")], writes=[bf("xc_o")])
                P.op("gpsimd", lambda e, so=szm_o[tb % 2]: e.tensor_tensor(xc_o[:], xc_o[:], so[:], ALU.mult),
                     reads=[bf("xc_o"), bf(szm_o[tb % 2].name)], writes=[bf("xc_o")])
                P.dma(xcT_d[tb * 4:(tb + 1) * 4].rearrange("c p j t -> p c (j t)"),
                      xc_o[:].rearrange("p c j t -> p c (j t)"), reads=[bf("xc_o")], writes=[bf("xcT_d")])
                for which, dst, dd, scale in ((0, q_o, qT_d, 512.0 ** -0.5), (1, k_o, kT_d, 1.0)):
                    for j in range(4):
                        pp = ps_cv[cv_i[0] % 2]
                        cv_i[0] += 1
                        P.mm([lambda e, j=j, pp=pp, which=which: e.matmul(pp[:], bdb[:, which, j, :], xc[:, j, :],
                                                                           start=True, stop=True)],
                             reads=[bf("bdb"), bf("xc")], writes=[bf(pp.name)])
                        P.op("scalar" if j % 2 == 0 else "vector",
                             (lambda e, j=j, pp=pp, dst=dst, scale=scale:
                              e.activation(out=dst[:, :, j, :], in_=pp[:].rearrange("p (c t) -> p c t", t=128),
                                           func=AF.Copy, scale=scale)) if j % 2 == 0 else
                             (lambda e, j=j, pp=pp, dst=dst, scale=scale:
                              e.tensor_scalar(dst[:, :, j, :], pp[:].rearrange("p (c t) -> p c t", t=128),
                                              scale, None, ALU.mult)),
                             reads=[bf(pp.name)], writes=[bf(dst.name)])
                    P.dma(dd[tb * 4:(tb + 1) * 4].rearrange("c p j t -> p c (j t)"),
                          dst[:].rearrange("p c j t -> p c (j t)"), reads=[bf(dst.name)], writes=[bf(dst.name + "_d")])
                for which, dst, dd, srcname in ((1, ktm_o, ktm_d, "xc"), (2, vtm_o, vtm_d, wname)):
                    for tt in range(4):
                        pp = ps_tm[tm_i[0] % 2]
                        tm_i[0] += 1
                        if which == 1:
                            fns = [(lambda e, j=j, tt=tt, pp=pp: e.matmul(
                                pp[:, j * 128:(j + 1) * 128], xc[:, j, tt * 128:(tt + 1) * 128], bdb[:, 1, j, :],
                                start=True, stop=True)) for j in range(4)]
                        else:
                            fns = [(lambda e, j=j, tt=tt, pp=pp: e.matmul(
                                pp[:, j * 128:(j + 1) * 128], win[:, j, 2 + tt * 128:2 + (tt + 1) * 128],
                                bdb[:, 2, j, :], start=True, stop=True)) for j in range(4)]
                        P.mm(fns, reads=[bf("bdb"), bf(srcname)], writes=[bf(pp.name)])
                        if tt % 2 == 0:
                            P.op("scalar", lambda e, tt=tt, pp=pp, dst=dst: e.activation(out=dst[:, tt, :], in_=pp[:],
                                                                                         func=AF.Copy),
                                 reads=[bf(pp.name)], writes=[bf(dst.name)])
                        else:
                            P.op("vector", lambda e, tt=tt, pp=pp, dst=dst: e.tensor_copy(dst[:, tt, :], pp[:]),
                                 reads=[bf(pp.name)], writes=[bf(dst.name)])
                    P.dma(dd[tb * 512:(tb + 1) * 512, :].rearrange("(tt p) c -> p tt c", p=128), dst[:],
                          reads=[bf(dst.name)], writes=[bf(dst.name + "_d")])
                P.mm([(lambda e, j=j: e.matmul(ps_sm[0:16, :], Wgc[:, j, :], xc[:, j, :], start=(j == 0), stop=False))
                      for j in range(4)] +
                     [(lambda e, j=j: e.matmul(ps_sm[0:16, :], Wgm[:, j, :], win[:, j, 2:514], start=False,
                                               stop=(j == 3))) for j in range(4)],
                     reads=[bf("Wgc"), bf("Wgm"), bf("xc"), bf(wname)], writes=[bf("ps_sm")])
                P.op("vector", lambda e: e.tensor_copy(gp_o[:], ps_sm[0:16, :]), reads=[bf("ps_sm")],
                     writes=[bf("gp_o")])
                P.dma(gp_d[:, tb * 512:(tb + 1) * 512], gp_o[:], reads=[bf("gp_o")], writes=[bf("gp_d")])

            for tb in range(NBR):
                X = xb[tb % 2]
                R = Rbc[tb % 2]
                RC = Rcol[tb % 2]
                if tb == 0:
                    xload(xT_d, 0, xs, "")
                xblock(xT_d, tb * 512, xs, X, sq, R, RC, ps_st, ps_sm, "", True,
                       after_cast=(lambda tb=tb: xload(xT_d, (tb + 1) * 512, xs, "")) if tb + 1 < NBR else None)
                win = xmw[tb % 3]
                wname = win.name
                for cc in range(10):
                    pp = ps_pj[pj_i[0] % 2]
                    pj_i[0] += 1
                    P.mm([(lambda e, k=k, cc=cc, pp=pp, X=X: e.matmul(pp[:], Wb[:, k, cc * 128:(cc + 1) * 128],
                                                                       X[:, k, :], start=(k == 0), stop=(k == 7)))
                          for k in range(8)],
                         reads=[bf("Wb"), bf(X.name)], writes=[bf(pp.name)])
                    if cc < 2 or cc >= 6:
                        tf = tmpf[tf_i[0] % 2]
                        tf_i[0] += 1
                        P.op("vector", lambda e, pp=pp, tf=tf, R=R: e.tensor_tensor(tf[:], pp[:], R[:], ALU.mult),
                             reads=[bf(pp.name), bf(R.name)], writes=[bf(tf.name)])
                        if cc < 2:
                            P.op("scalar", lambda e, cc=cc, tf=tf: e.activation(out=szf_o[:, cc, :], in_=tf[:],
                                                                                func=AF.Silu),
                                 reads=[bf(tf.name)], writes=[bf("szf_o")])
                        else:
                            j = cc - 6
                            P.op("scalar", lambda e, j=j, tf=tf, so=szm_o[tb % 2]: e.activation(
                                out=so[:, :, j, :], in_=tf[:].rearrange("p (c t) -> p c t", t=128), func=AF.Silu),
                                reads=[bf(tf.name)], writes=[bf(szm_o[tb % 2].name)])
                    else:
                        j = cc - 2
                        P.op("vector", lambda e, j=j, pp=pp, R=R, win=win: e.tensor_tensor(
                            win[:, j, 2:514], pp[:], R[:], ALU.mult),
                            reads=[bf(pp.name), bf(R.name)], writes=[bf(wname)])
                P.dma(szfT_d.rearrange("(j p) t -> p j t", p=128)[:, :, tb * 512:(tb + 1) * 512], szf_o[:],
                      reads=[bf("szf_o")], writes=[bf("szfT_d")])
                P.op("gpsimd", lambda e, so=szm_o[tb % 2]: e.tensor_tensor(
                    w1_o[:], so[:], hnw[:, None, :, None].broadcast_to([128, 4, 4, 128]), ALU.mult),
                    reads=[bf(szm_o[tb % 2].name), bf("hnw")], writes=[bf("w1_o")])
                P.dma(szmT_d[tb * 4:(tb + 1) * 4].rearrange("c p j t -> p c (j t)"),
                      w1_o[:].rearrange("p c j t -> p c (j t)"), reads=[bf("w1_o")], writes=[bf("szmT_d")])
                if tb == 0:
                    P.op("gpsimd", lambda e, win=win: e.memset(win[:, :, 0:2], 0.0), writes=[bf(wname)])
                else:
                    prev = xmw[(tb - 1) % 3]
                    P.op("gpsimd", lambda e, win=win, prev=prev: e.tensor_copy(win[:, :, 0:2], prev[:, :, 512:514]),
                         reads=[bf(prev.name)], writes=[bf(wname)])
                    P.op("gpsimd", lambda e, win=win, prev=prev: e.tensor_copy(prev[:, :, 514:516], win[:, :, 2:4]),
                         reads=[bf(wname)], writes=[bf(prev.name)])
                if tb == NB - 1:
                    P.op("gpsimd", lambda e, win=win: e.memset(win[:, :, 514:516], 0.0), writes=[bf(wname)])
                for tt in range(4):
                    for which in range(2):
                        pp = ps_tm[tm_i[0] % 2]
                        tm_i[0] += 1
                        if which == 0:
                            P.mm([(lambda e, k=k, tt=tt, pp=pp, X=X: e.matmul(
                                pp[:], X[:, k, tt * 128:(tt + 1) * 128], Wfz[:, k, :], start=(k == 0), stop=(k == 7)))
                                for k in range(8)],
                                reads=[bf("Wfz"), bf(X.name)], writes=[bf(pp.name)])
                            P.op("scalar", lambda e, tt=tt, pp=pp, RC=RC: e.activation(
                                out=z_o[:, tt, :], in_=pp[:], func=AF.Copy, scale=RC[:, tt:tt + 1]),
                                reads=[bf(pp.name), bf(RC.name)], writes=[bf("z_o")])
                        else:
                            P.mm([(lambda e, k=k, tt=tt, pp=pp, X=X: e.matmul(
                                pp[:], X[:, k, tt * 128:(tt + 1) * 128], Wb[:, k, 1280:1792], start=(k == 0),
                                stop=(k == 7))) for k in range(8)],
                                reads=[bf("Wb"), bf(X.name)], writes=[bf(pp.name)])
                            P.op("scalar", lambda e, tt=tt, pp=pp, RC=RC: e.activation(
                                out=so_o[:, tt, :], in_=pp[:], func=AF.Sigmoid, scale=RC[:, tt:tt + 1]),
                                reads=[bf(pp.name), bf(RC.name)], writes=[bf("so_o")])
                P.dma(Zs_d[tb * 512:(tb + 1) * 512, :].rearrange("(tt p) c -> p tt c", p=128), z_o[:],
                      reads=[bf("z_o")], writes=[bf("Zs_d")])
                P.dma(sigo_d[tb * 512:(tb + 1) * 512, :].rearrange("(tt p) c -> p tt c", p=128), so_o[:],
                      reads=[bf("so_o")], writes=[bf("sigo_d")])
                if tb >= 1:
                    conv_block(tb - 1)
            if not skip1:
                conv_block(NB - 1)
            P.barrier()
        if stop_after <= 1:
            P.run()
            return nc

        def L(*names):
            return [bf(n) for n in names]

        def V(fn, r, w):
            P.op("vector", fn, reads=L(*r), writes=L(*w))

        def A(fn, r, w):
            P.op("scalar", fn, reads=L(*r), writes=L(*w))

        def G(fn, r, w):
            P.op("gpsimd", fn, reads=L(*r), writes=L(*w))

        with ExitStack() as ph2:
            Abc = [sbt(ph2, "Abc%d" % d, [128, S], F32) for d in range(2)]
            cols = [sbt(ph2, "cols%d" % d, [128, 4, 64], F32) for d in range(2)]
            dbc = [sbt(ph2, "dbc%d" % d, [128, 64], F32) for d in range(2)]
            ubc = [sbt(ph2, "ubc%d" % d, [128, 64], F32) for d in range(2)]
            masks = sbt(ph2, "masks", [128, 2, 128], F32)
            P.dma(masks[:], mask_d, writes=L("masks"))

            def exchange_f(tg):
                if nocoll:
                    for r in range(4):
                        P.dma(ufout_d[tg, r * 256:(r + 1) * 256, :], ufin_d[tg], reads=L("ufin%d" % tg),
                              writes=L("ufout%d" % tg))
                else:
                    P.coll(lambda e, tg=tg: e.collective_compute(
                        "AllGather", ALU.bypass, replica_groups=GROUPS, ins=[ufin_d[tg].opt()],
                        outs=[ufout_d[tg].opt()]), reads=L("ufin%d" % tg), writes=L("ufout%d" % tg))

            with ExitStack() as phf:
                if nocoll:
                    P.dma(gr_d, gp_d, reads=L("gp_d"), writes=L("gr_d"))
                else:
                    P.coll(lambda e: e.collective_compute("AllReduce", ALU.add, replica_groups=GROUPS,
                                                          ins=[gp_d.opt()], outs=[gr_d.opt()]),
                           reads=L("gp_d"), writes=L("gr_d"))
                if dbg:
                    P.dma(dbg_gr, gr_d, reads=L("gr_d"), writes=L("dbg_gr"))
                Zt = [sbt(phf, "Zt%d" % i, [128, 512], BF16) for i in range(3)]
                M1t = [sbt(phf, "M1t%d" % i, [128, 384], BF16) for i in range(3)]
                A1o = [sbt(phf, "A1o%d" % i, [128, 512], BF16) for i in range(2)]
                szf = sbt(phf, "szf", [128, 2, S], BF16)
                ufT = sbt(phf, "ufT", [128, 2, S], BF16)
                T2 = [sbt(phf, "T2_%d" % i, [128, 8, 256], BF16) for i in range(2)]
                cs2 = sbt(phf, "cs2", [128, 64], BF16)
                ps_f = [pst(phf, "ps_f%d" % i, [128, 512]) for i in range(2)]
                ps_y = [pst(phf, "ps_y%d" % i, [128, 512]) for i in range(2)]
                pg = pst(phf, "pg", [128, 512])
                pcol = pst(phf, "pcol", [128, 512])
                P.dma(cs2[:], cs2_d, writes=L("cs2"))
                P.dma(szf[:], szfT_d.rearrange("(j p) t -> p j t", p=128), reads=L("szfT_d"), writes=L("szf"))
                Zv = Zs_d.rearrange("(n1 n2) c -> n2 n1 c", n2=64)

                def f1_load(n2):
                    zt = Zt[n2 % 3]
                    mt = M1t[n2 % 3]
                    P.dma(zt[:], Zv[n2], reads=L("Zs_d"), writes=L(zt.name))
                    P.dma(mt[:], m1_d[n2], writes=L(mt.name))

                def f1_unit(n2):
                    zt = Zt[n2 % 3]
                    mt = M1t[n2 % 3]
                    pp = ps_f[n2 % 2]
                    ao = A1o[n2 % 2]
                    P.mm([lambda e: e.matmul(pp[:, 0:256], mt[:, 0:128], zt[:, 0:256], start=True, stop=False),
                          lambda e: e.matmul(pp[:, 0:256], mt[:, 128:256], zt[:, 256:512], start=False, stop=True),
                          lambda e: e.matmul(pp[:, 256:512], mt[:, 0:128], zt[:, 256:512], start=True, stop=False),
                          lambda e: e.matmul(pp[:, 256:512], mt[:, 256:384], zt[:, 0:256], start=False, stop=True)],
                         reads=L(zt.name, mt.name), writes=L(pp.name))
                    if n2 % 2 == 0:
                        A(lambda e: e.activation(out=ao[:], in_=pp[:], func=AF.Copy), [pp.name], [ao.name])
                    else:
                        V(lambda e: e.tensor_copy(ao[:], pp[:]), [pp.name], [ao.name])
                    P.dma(A1_d[n2], ao[:], reads=L(ao.name), writes=L("A1_d"))

                def f2_load(kb):
                    t2 = T2[kb % 2]
                    P.dma(t2[0:64, :, :], A1_d[:, kb * 8:(kb + 1) * 8, 0:256], reads=L("A1_d"), writes=L(t2.name))
                    P.dma(t2[64:128, :, :], A1_d[:, kb * 8:(kb + 1) * 8, 256:512], reads=L("A1_d"),
                          writes=L(t2.name))

                def f2_unit(kb):
                    t2 = T2[kb % 2]
                    for hh in range(2):
                        pp = ps_y[hh]
                        P.mm([(lambda e, j=j, pp=pp, hh=hh: e.matmul(pp[:, j * 64:(j + 1) * 64],
                                                                     t2[:, j, hh * 128:(hh + 1) * 128],
                                                                     cs2[:], start=True, stop=True)) for j in range(8)],
                             reads=L(t2.name, "cs2"), writes=L(pp.name))
                        ov = ufT[:, hh, :].rearrange("p (k2 k1) -> p k1 k2", k1=128)[:, kb * 8:(kb + 1) * 8, :]
                        sv = szf[:, hh, :].rearrange("p (k2 k1) -> p k1 k2", k1=128)[:, kb * 8:(kb + 1) * 8, :]
                        V(lambda e, ov=ov, sv=sv, pp=pp: e.tensor_tensor(
                            ov, pp[:].rearrange("p (j k) -> p j k", k=64), sv, ALU.mult),
                          [pp.name, "szf"], ["ufT"])

                Gt = sbt(phf, "Gt", [64, 16, 128], F32)
                sel = sbt(phf, "sel", [64, 4], F32)
                gb = sbt(phf, "gb", [64, 4], F32)
                ngb = sbt(phf, "ngb", [64, 4], F32)
                ustr = sbt(phf, "ustr", [64, 2, 64], F32)
                gt = {}
                for d in range(2):
                    for nm_ in ["I", "F", "E", "Lg", "Sg", "Bn", "a", "Mx", "Ag", "inter", "ws", "EM", "drep", "urep"]:
                        gt[(nm_, d)] = sbt(phf, "g%s%d" % (nm_, d), [64, 128], F32)
                    for nm_, shp in [("Tc", [64, 1]), ("Eo", [64, 1]), ("mxc", [64, 1]), ("mrow", [1, 64]),
                                     ("vrow", [1, 64]), ("urow", [1, 64]), ("uc", [64, 2]), ("nun", [64, 1]),
                                     ("dec", [64, 1])]:
                        gt[(nm_, d)] = sbt(phf, "g%s%d" % (nm_, d), shp, F32)

                def gates_units(d):
                    fwd = (d == 0)
                    T = lambda n: gt[(n, d)]
                    N = lambda n: gt[(n, d)].name

                    def rv(ap):
                        return ap if fwd else ap[:, ::-1]

                    def u0():
                        for dst, t in ((T("I"), 2 * d), (T("F"), 2 * d + 1)):
                            V(lambda e, dst=dst, t=t: e.tensor_scalar(dst[:], Gt[:, 4 * t, :], sel[:, 0:1], None,
                                                                      ALU.mult), ["Gt", "sel"], [dst.name])
                            for h in range(1, 4):
                                V(lambda e, dst=dst, t=t, h=h: e.scalar_tensor_tensor(
                                    dst[:], Gt[:, 4 * t + h, :], sel[:, h:h + 1], dst[:], ALU.mult, ALU.add),
                                  ["Gt", "sel", dst.name], [dst.name])
                        A(lambda e: e.activation(out=T("I")[:], in_=T("I")[:], func=AF.Identity,
                                                 bias=gb[:, 2 * d:2 * d + 1], scale=1.0), [N("I"), "gb"], [N("I")])
                        A(lambda e: e.activation(out=T("E")[:], in_=T("F")[:], func=AF.Exp,
                                                 bias=ngb[:, 2 * d + 1:2 * d + 2], scale=-1.0),
                          [N("F"), "ngb"], [N("E")])
                        A(lambda e: e.activation(out=T("Lg")[:], in_=T("E")[:], func=AF.Ln, bias=1.0, scale=1.0),
                          [N("E")], [N("Lg")])
                        V(lambda e: e.tensor_tensor_scan(rv(T("Sg")[:]), rv(onesf[0:64, :]), rv(T("Lg")[:]), 0.0,
                                                         ALU.mult, ALU.add), [N("Lg"), "onesf"], [N("Sg")])
                        V(lambda e: e.tensor_copy(T("Tc")[:], T("Sg")[:, 127:128] if fwd else T("Sg")[:, 0:1]),
                          [N("Sg")], [N("Tc")])
                        P.mm([lambda e: e.matmul(pg[0:64, 0:1], ustr[:, d, :], T("Tc")[:], start=True, stop=True)],
                             reads=L("ustr", N("Tc")), writes=L("pg"))

                    def u1():
                        V(lambda e: e.tensor_copy(T("Eo")[:], pg[0:64, 0:1]), ["pg"], [N("Eo")])
                        V(lambda e: e.tensor_scalar(T("Bn")[:], T("Sg")[:], T("Eo")[:, 0:1], None, ALU.add),
                          [N("Sg"), N("Eo")], [N("Bn")])
                        V(lambda e: e.tensor_tensor(T("a")[:], T("I")[:], T("Bn")[:], ALU.add),
                          [N("I"), N("Bn")], [N("a")])
                        V(lambda e: e.tensor_tensor_scan(rv(T("Mx")[:]), rv(onesf[0:64, :]), rv(T("a")[:]), 0.0,
                                                         ALU.mult, ALU.max), [N("a"), "onesf"], [N("Mx")])
                        V(lambda e: e.tensor_copy(T("mxc")[:], T("Mx")[:, 127:128] if fwd else T("Mx")[:, 0:1]),
                          [N("Mx")], [N("mxc")])
                        P.mm([lambda e: e.matmul(pg[0:1, 64:128], T("mxc")[:], identf[0:64, 0:64], start=True,
                                                 stop=True)], reads=L("identf", N("mxc")), writes=L("pg"))

                    def u2():
                        V(lambda e: e.tensor_copy(T("mrow")[:], pg[0:1, 64:128]), ["pg"], [N("mrow")])
                        V(lambda e: e.tensor_tensor_scan(rv(T("vrow")[:]), rv(onesf[0:1, 0:64]), rv(T("mrow")[:]), 0.0,
                                                         ALU.mult, ALU.max), [N("mrow"), "onesf"], [N("vrow")])
                        V(lambda e: e.memset(T("urow")[:], 0.0), [], [N("urow")])
                        if fwd:
                            V(lambda e: e.tensor_copy(T("urow")[:, 1:64], T("vrow")[:, 0:63]), [N("vrow"), N("urow")],
                              [N("urow")])
                        else:
                            V(lambda e: e.tensor_copy(T("urow")[:, 0:63], T("vrow")[:, 1:64]), [N("vrow"), N("urow")],
                              [N("urow")])
                        P.mm([lambda e: e.matmul(pg[0:64, 130:131], T("urow")[:], onesf[0:1, 0:1], start=True,
                                                 stop=True),
                              lambda e: e.matmul(pg[0:64, 131:132], T("vrow")[:], onesf[0:1, 0:1], start=True,
                                                 stop=True)],
                             reads=L("onesf", N("urow"), N("vrow")), writes=L("pg"))

                    def u3():
                        V(lambda e: e.tensor_copy(T("uc")[:], pg[0:64, 130:132]), ["pg"], [N("uc")])
                        V(lambda e: e.tensor_scalar(T("Ag")[:], T("Mx")[:], T("uc")[:, 0:1], None, ALU.max),
                          [N("Mx"), N("uc")], [N("Ag")])
                        V(lambda e: e.tensor_scalar(T("nun")[:], T("uc")[:, 1:2], -1.0, None, ALU.mult),
                          [N("uc")], [N("nun")])
                        A(lambda e: e.activation(out=T("inter")[:], in_=T("Ag")[:], func=AF.Exp,
                                                 bias=T("uc")[:, 0:1], scale=-1.0), [N("Ag"), N("uc")], [N("inter")])
                        A(lambda e: e.activation(out=T("ws")[:], in_=T("a")[:], func=AF.Exp, bias=T("nun")[:, 0:1],
                                                 scale=1.0), [N("a"), N("nun")], [N("ws")])
                        A(lambda e: e.activation(out=T("dec")[:], in_=T("uc")[:, 0:1], func=AF.Exp,
                                                 bias=T("nun")[:, 0:1], scale=1.0), [N("uc"), N("nun")], [N("dec")])
                        V(lambda e: e.tensor_tensor(T("EM")[:], T("Bn")[:], T("Ag")[:], ALU.subtract),
                          [N("Bn"), N("Ag")], [N("EM")])
                        A(lambda e: e.activation(out=T("EM")[:], in_=T("EM")[:], func=AF.Exp), [N("EM")], [N("EM")])
                        V(lambda e: e.tensor_scalar(T("drep")[:], onesf[0:64, :], T("dec")[:, 0:1], None, ALU.mult),
                          ["onesf", N("dec")], [N("drep")])
                        V(lambda e: e.tensor_scalar(T("urep")[:], onesf[0:64, :], T("uc")[:, 0:1], None, ALU.mult),
                          ["onesf", N("uc")], [N("urep")])
                        P.mm([(lambda e, qi=qi, nm_=nm_: e.matmul(pcol[:, qi * 64:(qi + 1) * 64], T(nm_)[:],
                                                                   identf[0:64, 0:64], start=True, stop=True))
                              for qi, nm_ in enumerate(["a", "ws", "inter", "EM", "drep", "urep"])],
                             reads=L("identf", N("a"), N("ws"), N("inter"), N("EM"), N("drep"), N("urep")),
                             writes=L("pcol"))
                        P.dma(arow_d[d:d + 1, :].rearrange("o (c p) -> (o c) p", p=128), T("Ag")[:],
                              reads=L(N("Ag")), writes=L("arow_d%d" % d))

                    def u4():
                        V(lambda e: e.tensor_copy(cols[d][:].rearrange("p q c -> p (q c)"), pcol[:, 0:256]),
                          ["pcol"], [cols[d].name])
                        V(lambda e: e.tensor_copy(dbc[d][:], pcol[:, 256:320]), ["pcol"], [dbc[d].name])
                        V(lambda e: e.tensor_copy(ubc[d][:], pcol[:, 320:384]), ["pcol"], [ubc[d].name])
                        P.dma(Abc[d][:], arow_d[d:d + 1, :].partition_broadcast(128), reads=L("arow_d%d" % d),
                              writes=L(Abc[d].name))
                        if dbg:
                            P.dma(dbg_cols[d, :, 0:4, :], cols[d][:], reads=L(cols[d].name), writes=L("dbg_cols"))
                            P.dma(dbg_cols[d, :, 4, :], dbc[d][:], reads=L(dbc[d].name), writes=L("dbg_cols"))

                    return [u0, u1, u2, u3, u4]

                NF1 = 64 if "a" in parts else 0
                for n2 in range(min(2, NF1)):
                    f1_load(n2)
                for n2 in range(NF1):
                    if n2 + 2 < NF1:
                        f1_load(n2 + 2)
                    f1_unit(n2)
                P.dma(Gt[:], gr_d.rearrange("g (c p) -> c g p", p=128), reads=L("gr_d"), writes=L("Gt"))
                P.dma(sel[:], sel_d, writes=L("sel"))
                P.dma(gb[:], gb_d, writes=L("gb"))
                P.dma(ustr[:], ustr_d, writes=L("ustr"))
                V(lambda e: e.tensor_scalar(ngb[:], gb[:], -1.0, None, ALU.mult), ["gb"], ["ngb"])
                gu = (gates_units(0) + gates_units(1)) if "g" in parts else []
                gi = 0
                NF2 = 16 if "b" in parts else 0
                if NF2:
                    f2_load(0)
                for kb in range(NF2):
                    if gi < len(gu):
                        gu[gi]()
                        gi += 1
                    if kb + 1 < NF2:
                        f2_load(kb + 1)
                    f2_unit(kb)
                while gi < len(gu):
                    gu[gi]()
                    gi += 1
                for j in range(2):
                    P.dma(ufin_d[:, j * 128:(j + 1) * 128, :].rearrange("tg p t -> p tg t"),
                          ufT[:, j, :].rearrange("p (tg t) -> p tg t", t=1024), reads=L("ufT"),
                          writes=L(*["ufin%d" % tg for tg in range(8)]))
                P.barrier(coll=False)
            if stop_after <= 2:
                if dbg:
                    P.barrier()
                P.run()
                return nc

            with ExitStack() as phm:
                D2 = range(2)
                Cbf = [sbt(phm, "Cbf%d" % d, [128, 4, 512], BF16) for d in D2]
                n32 = [sbt(phm, "n32_%d" % d, [128, 4], F32) for d in D2]
                nbf = [sbt(phm, "nbf%d" % d, [128, 4], BF16) for d in D2]

                def two(name, shape, dt, n=2):
                    return [[sbt(phm, "%s%d_%d" % (name, d, i), shape, dt) for i in range(n)] for d in D2]

                def one(name, shape, dt):
                    return [sbt(phm, "%s%d" % (name, d), shape, dt) for d in D2]

                qTc = two("qTc", [128, 4, 128], BF16, 3)
                kTc = two("kTc", [128, 4, 128], BF16, 3)
                ktc = two("ktc", [128, 512], BF16, 3)
                vtc = two("vtc", [128, 512], BF16, 3)
                hoth = two("hoth", [128, 512], BF16, 3)
                sgc = two("sgc", [128, 512], BF16, 3)
                xcc = two("xcc", [128, 4, 128], BF16, 3)
                szc = two("szc", [128, 4, 128], BF16, 3)
                hbuf = two("hbuf", [128, 512], BF16, 3)
                u_o = two("u_o", [128, 4, 128], BF16)
                tmpD = one("tmpD", [128, 128], F32)
                Dm = one("Dm", [128, 128], F32)
                ibc = one("ibc", [128, 128], BF16)
                qs = one("qs", [128, 4, 128], BF16)
                PT = one("PT", [128, 128], BF16)
                kw = two("kw", [128, 512], BF16)
                decI = two("decI", [128, 128], BF16)
                den = one("den", [128, 2], F32)
                rr = one("rr", [128, 1], F32)
                hs = one("hs", [128, 512], F32)
                junk = one("junk", [128, 512], BF16)
                ssq = one("ssq", [128, 1], F32)
                hn = one("hn", [128, 512], BF16)
                e2 = one("e2", [128, 4, 128], F32)
                e3 = one("e3", [128, 4, 128], F32)
                ps_misc = pst(phm, "ps_misc", [128, 512])
                ps_tr = pst(phm, "ps_tr", [128, 1024], BF16)
                ps_num = [pst(phm, "ps_num%d" % d, [128, 512]) for d in D2]
                ps_dc = [[pst(phm, "ps_dc%d_%d" % (d, i), [128, 512]) for i in range(2)] for d in D2]

                for d in D2:
                    V(lambda e, d=d: e.memset(Cbf[d][:], 0.0), [], ["Cbf%d_%d" % (d, j) for j in range(4)])
                    V(lambda e, d=d: e.memset(n32[d][:], 0.0), [], [n32[d].name])
                    V(lambda e, d=d: e.memset(nbf[d][:], 0.0), [], [nbf[d].name])

                def load_step(d, c, it):
                    sl = it % 3
                    P.dma(qTc[d][sl][:].rearrange("p j t -> p (j t)"), qT_d[c].rearrange("p j t -> p (j t)"),
                          reads=L("q_o_d"), writes=L(qTc[d][sl].name))
                    P.dma(kTc[d][sl][:].rearrange("p j t -> p (j t)"), kT_d[c].rearrange("p j t -> p (j t)"),
                          reads=L("k_o_d"), writes=L(kTc[d][sl].name))
                    P.dma(ktc[d][sl][:], ktm_d[c * 128:(c + 1) * 128, :], reads=L("ktm_o_d"),
                          writes=L(ktc[d][sl].name))
                    P.dma(vtc[d][sl][:], vtm_d[c * 128:(c + 1) * 128, :], reads=L("vtm_o_d"),
                          writes=L(vtc[d][sl].name))
                    P.dma(sgc[d][sl][:], sigo_d[c * 128:(c + 1) * 128, :], reads=L("sigo_d"),
                          writes=L(sgc[d][sl].name))

                def load_ep(d, c, it):
                    if it >= 32:
                        s3 = it % 3
                        P.dma(hoth[d][s3][:], hst_d[1 - d, c], reads=L("hst_d%d" % (1 - d)),
                              writes=L(hoth[d][s3].name))
                        P.dma(xcc[d][s3][:].rearrange("p j t -> p (j t)"), xcT_d[c].rearrange("p j t -> p (j t)"),
                              reads=L("xcT_d"), writes=L(xcc[d][s3].name))
                        P.dma(szc[d][s3][:].rearrange("p j t -> p (j t)"), szmT_d[c].rearrange("p j t -> p (j t)"),
                              reads=L("szmT_d"), writes=L(szc[d][s3].name))

                def exchange(tg):
                    if nocoll:
                        for r in range(4):
                            P.dma(uout_d[tg, r * 512:(r + 1) * 512, :], uin_d[tg], reads=L("uin%d" % tg),
                                  writes=L("uout%d" % tg))
                    else:
                        P.coll(lambda e: e.collective_compute("AllGather", ALU.bypass, replica_groups=GROUPS,
                                                              ins=[uin_d[tg].opt()], outs=[uout_d[tg].opt()]),
                               reads=L("uin%d" % tg), writes=L("uout%d" % tg))

                def CBn(d):
                    return ["Cbf%d_%d" % (d, j) for j in range(4)]

                def stage1(d, c, it):
                    sl = it % 3
                    qT, kT, kt = qTc[d][sl], kTc[d][sl], ktc[d][sl]
                    o = d * 256
                    P.mm([(lambda e, j=j: e.matmul(ps_misc[:, o:o + 128], kT[:, j, :], qT[:, j, :], start=(j == 0),
                                                   stop=(j == 3))) for j in range(4)],
                         reads=L(kT.name, qT.name), writes=L("ps_misc"))
                    V(lambda e: e.scalar_tensor_tensor(tmpD[d][:], Abc[d][:, c * 128:(c + 1) * 128],
                                                       cols[d][:, 0, c:c + 1], masks[:, d, :], ALU.subtract, ALU.max),
                      [Abc[d].name, cols[d].name, "masks"], [tmpD[d].name])
                    A(lambda e: e.activation(out=Dm[d][:], in_=tmpD[d][:], func=AF.Exp, scale=-1.0),
                      [tmpD[d].name], [Dm[d].name])
                    A(lambda e: e.activation(out=ibc[d][:], in_=Abc[d][:, c * 128:(c + 1) * 128], func=AF.Exp,
                                             bias=ubc[d][:, c:c + 1], scale=-1.0),
                      [Abc[d].name, ubc[d].name], [ibc[d].name])
                    V(lambda e: e.tensor_tensor(PT[d][:], ps_misc[:, o:o + 128], Dm[d][:], ALU.mult),
                      ["ps_misc", Dm[d].name], [PT[d].name])
                    V(lambda e: e.tensor_tensor(qs[d][:], qT[:], ibc[d][:, None, :].broadcast_to([128, 4, 128]),
                                                ALU.mult), [qT.name, ibc[d].name], [qs[d].name])
                    kw_, dI_ = kw[d][it % 2], decI[d][it % 2]
                    A(lambda e: e.activation(out=kw_[:], in_=kt[:], func=AF.Copy, scale=cols[d][:, 1, c:c + 1]),
                      [kt.name, cols[d].name], [kw_.name])
                    A(lambda e: e.activation(out=dI_[:], in_=identb[:], func=AF.Copy, scale=dbc[d][:, c:c + 1]),
                      ["identb", dbc[d].name], [dI_.name])

                def dc_round(d, it, rnd):
                    vt = vtc[d][it % 3]
                    kw_, dI_ = kw[d][it % 2], decI[d][it % 2]
                    CB = CBn(d)
                    for jj in range(2):
                        j = 2 * rnd + jj
                        pd = ps_dc[d][jj]
                        P.mm([lambda e, j=j, pd=pd: e.matmul(pd[:], kw_[:, j * 128:(j + 1) * 128], vt[:],
                                                             start=True, stop=False),
                              lambda e, j=j, pd=pd: e.matmul(pd[:], dI_[:], Cbf[d][:, j, :], start=False,
                                                             stop=True)],
                             reads=L(kw_.name, vt.name, dI_.name, CB[j]), writes=L(pd.name))

                def dc_evac(d, rnd):
                    CB = CBn(d)
                    for jj in range(2):
                        j = 2 * rnd + jj
                        pd = ps_dc[d][jj]
                        if jj == 0:
                            A(lambda e, j=j, pd=pd: e.activation(out=Cbf[d][:, j, :], in_=pd[:], func=AF.Copy),
                              [pd.name], [CB[j]])
                        else:
                            V(lambda e, j=j, pd=pd: e.tensor_copy(Cbf[d][:, j, :], pd[:]), [pd.name], [CB[j]])

                def stage2a(d, c, it):
                    sl = it % 3
                    vt = vtc[d][sl]
                    kw_ = kw[d][it % 2]
                    o = d * 256
                    CB = CBn(d)
                    P.mm([lambda e: e.matmul(ps_num[d][:], PT[d][:], vt[:], start=True, stop=False)] +
                         [(lambda e, j=j: e.matmul(ps_num[d][:], qs[d][:, j, :], Cbf[d][:, j, :], start=False,
                                                   stop=(j == 3))) for j in range(4)] +
                         [lambda e: e.matmul(ps_misc[:, o + 128:o + 129], PT[d][:], onesb[:, 0:1], start=True,
                                             stop=False)] +
                         [(lambda e, j=j: e.matmul(ps_misc[:, o + 128:o + 129], qs[d][:, j, :], nbf[d][:, j:j + 1],
                                                   start=False, stop=(j == 3))) for j in range(4)] +
                         [(lambda e, j=j: e.matmul(ps_misc[:, o + 136 + j:o + 137 + j],
                                                   kw_[:, j * 128:(j + 1) * 128],
                                                   onesb[:, 0:1], start=True, stop=True)) for j in range(4)],
                         reads=L(PT[d].name, vt.name, qs[d].name, nbf[d].name, "onesb", kw_.name, *CB),
                         writes=L(ps_num[d].name, "ps_misc"))
                    dc_round(d, it, 0)

                def stage3a(d, c, it):
                    sl = it % 3
                    o = d * 256
                    hb = hbuf[d][it % 3]
                    V(lambda e: e.tensor_copy(den[d][:, 0:1], ps_misc[:, o + 128:o + 129]), ["ps_misc"],
                      [den[d].name])
                    V(lambda e: e.tensor_scalar(den[d][:, 1:2], den[d][:, 0:1], -1.0, None, ALU.mult),
                      [den[d].name], [den[d].name])
                    V(lambda e: e.tensor_tensor(den[d][:, 0:1], den[d][:, 0:1], den[d][:, 1:2], ALU.max),
                      [den[d].name], [den[d].name])
                    V(lambda e: e.tensor_tensor(den[d][:, 0:1], den[d][:, 0:1], cols[d][:, 3, c:c + 1], ALU.max),
                      [den[d].name, cols[d].name], [den[d].name])
                    V(lambda e: e.reciprocal(rr[d][:], den[d][:, 0:1]), [den[d].name], [rr[d].name])
                    sg = sgc[d][sl]
                    V(lambda e: e.scalar_tensor_tensor(hb[:], ps_num[d][:], rr[d][:, 0:1], sg[:], ALU.mult,
                                                       ALU.mult), [ps_num[d].name, rr[d].name, sg.name], [hb.name])
                    V(lambda e: e.scalar_tensor_tensor(n32[d][:], n32[d][:], dbc[d][:, c:c + 1],
                                                       ps_misc[:, o + 136:o + 140], ALU.mult, ALU.add),
                      [n32[d].name, dbc[d].name, "ps_misc"], [n32[d].name])
                    V(lambda e: e.tensor_copy(nbf[d][:], n32[d][:]), [n32[d].name], [nbf[d].name])
                    dc_evac(d, 0)

                def stage2b(d, c, it):
                    dc_round(d, it, 1)

                def stage3b(d, c, it):
                    hb = hbuf[d][it % 3]
                    dc_evac(d, 1)
                    if it < 32:
                        P.dma(hst_d[d, c], hb[:], reads=L(hb.name), writes=L("hst_d%d" % d))

                def ep1(d, c, it):
                    hb, ho = hbuf[d][it % 3], hoth[d][it % 3]
                    V(lambda e: e.tensor_tensor(hs[d][:], hb[:], ho[:], ALU.add), [hb.name, ho.name], [hs[d].name])
                    A(lambda e: e.activation(out=junk[d][:], in_=hs[d][:], func=AF.Square, accum_out=ssq[d][:]),
                      [hs[d].name], [junk[d].name, ssq[d].name])

                def ep2(d, c, it):
                    A(lambda e: e.activation(out=ssq[d][:], in_=ssq[d][:], func=AF.Ln, bias=epsc[:, 0:1],
                                             scale=1.0 / 512.0), [ssq[d].name, "epsc"], [ssq[d].name])
                    A(lambda e: e.activation(out=ssq[d][:], in_=ssq[d][:], func=AF.Exp, scale=-0.5),
                      [ssq[d].name], [ssq[d].name])
                    A(lambda e: e.activation(out=hn[d][:], in_=hs[d][:], func=AF.Copy, scale=ssq[d][:, 0:1]),
                      [hs[d].name, ssq[d].name], [hn[d].name])

                def ep3(d, c, it):
                    trv = ps_tr[:, d * 512:(d + 1) * 512]
                    P.mm([(lambda e, j=j: e.transpose(trv[:, j * 128:(j + 1) * 128],
                                                      hn[d][:, j * 128:(j + 1) * 128], identb[:]))
                          for j in range(4)],
                         reads=L(hn[d].name, "identb"), writes=L("ps_tr"))

                def ep4(d, c, it):
                    trv = ps_tr[:, d * 512:(d + 1) * 512]
                    xcT_c, sz = xcc[d][it % 3], szc[d][it % 3]
                    V(lambda e: e.tensor_tensor(e2[d][:], trv.rearrange("p (j t) -> p j t", t=128), sz[:], ALU.mult),
                      ["ps_tr", sz.name], [e2[d].name])
                    uo = u_o[d][it % 2]
                    V(lambda e: e.tensor_tensor(uo[:], e2[d][:], xcT_c[:], ALU.add), [e2[d].name, xcT_c.name],
                      [uo.name])
                    P.dma(uin_d[c // 4, :, (c % 4) * 128:(c % 4 + 1) * 128].rearrange("(j p) t -> p j t", p=128),
                          uo[:], reads=L(uo.name), writes=L("uin%d" % (c // 4)))
                    if (d == 0 and c % 4 == 3) or (d == 1 and c % 4 == 0):
                        exchange(c // 4)

                NIT = 64
                load_step(0, 0, 0)
                load_step(1, 63, 0)
                load_step(0, 1, 1)
                load_step(1, 62, 1)
                for it in range(NIT):
                    late = (it + 1 == 32)
                    if it + 2 < NIT:
                        load_step(0, it + 2, it + 2)
                        load_step(1, 61 - it, it + 2)
                    if it + 1 < NIT and not late:
                        load_ep(0, it + 1, it + 1)
                        load_ep(1, 62 - it, it + 1)
                    cc_ = (it, 63 - it)
                    nc_ = (it + 1, 62 - it)
                    pc_ = (it - 1, 64 - it)
                    if it == 0:
                        for d in D2:
                            stage1(d, cc_[d], 0)
                    for stg_fn, ep_fn in ((stage2a, ep1), (stage3a, ep2), ("next1", ep3), (stage2b, ep4),
                                          (stage3b, None)):
                        for d in D2:
                            if stg_fn == "next1":
                                if it + 1 < NIT:
                                    stage1(d, nc_[d], it + 1)
                            else:
                                stg_fn(d, cc_[d], it)
                        if ep_fn is not None and it - 1 >= 32:
                            for d in D2:
                                ep_fn(d, pc_[d], it - 1)
                    if late:
                        load_ep(0, it + 1, it + 1)
                        load_ep(1, 62 - it, it + 1)
                    if it < 32 and it % 4 == 1:
                        exchange_f(it // 4)
                for ep_fn in (ep1, ep2, ep3, ep4):
                    for d in D2:
                        ep_fn(d, (63, 0)[d], 63)
                P.barrier(coll=False)
                if dbg:
                    P.dma(dbg_uown, uin_d, reads=L(*["uin%d" % tg for tg in range(16)]), writes=L("dbg_uown"))
                    P.dma(dbg_h[0, 0:32], hst_d[0, 0:32], reads=L("hst_d0", "hst_d1"), writes=L("dbg_h"))
                    P.dma(dbg_h[1, 32:64], hst_d[1, 32:64], reads=L("hst_d0", "hst_d1"), writes=L("dbg_h"))
                    P.barrier()
        if stop_after <= 3:
            P.run()
            return nc

        with ExitStack() as ph3:
            NT = 256
            NB3 = 2048 // NT
            Wg = sbt(ph3, "Wg", [128, 8, 2048], BF16)
            Wfo = sbt(ph3, "Wfo", [128, 8, 1024], BF16)
            Wml = sbt(ph3, "Wml", [128, 16, 1024], BF16)
            Wout = sbt(ph3, "Wout", [128, 8, 1024], BF16)
            stg = [sbt(ph3, "stg%d" % i, [128, 2048], F32) for i in range(2)]
            xb3 = [sbt(ph3, "xb3_%d" % i, [128, 8, NT], BF16) for i in range(2)]
            sq3 = sbt(ph3, "sq3", [128, 8, NT], BF16)
            R3 = [sbt(ph3, "R3_%d" % i, [128, NT], F32) for i in range(2)]
            utb = [sbt(ph3, "ut%d" % i, [128, 24, NT], BF16) for i in range(2)]
            gT = sbt(ph3, "gT", [128, 16, NT], BF16)
            yT = sbt(ph3, "yT", [128, 8, NT], BF16)
            tf3 = [sbt(ph3, "tf3_%d" % i, [128, NT], F32) for i in range(2)]
            ta = sbt(ph3, "ta", [128, NT], F32)
            tb_ = sbt(ph3, "tb_", [128, NT], F32)
            xres = sbt(ph3, "xres", [128, D], F32)
            xo = sbt(ph3, "xo", [128, D], F32)
            yo = sbt(ph3, "yo", [128, D], F32)
            fnw = sbt(ph3, "fnw", [128, D], F32)
            junk3 = sbt(ph3, "junk3", [128, D], BF16)
            ssq3 = sbt(ph3, "ssq3", [128, 1], F32)
            qoffs = sbt(ph3, "qoffs", [1, 2], I32)
            qreg = ph3.enter_context(nc.sync.register("qreg"))
            qreg2 = ph3.enter_context(nc.sync.register("qreg2"))
            ps_g = [pst(ph3, "ps_g%d" % i, [128, 512]) for i in range(2)]
            ps_a = pst(ph3, "ps_a", [128, 512])
            ps_b = pst(ph3, "ps_b", [128, 512])
            ps_o = [pst(ph3, "ps_o%d" % i, [128, 512]) for i in range(2)]
            ps_s3 = pst(ph3, "ps_s3", [128, 512])

            P.dma(qoffs[:], qoff_d, writes=L("qoffs"))
            P._emit_waits("sync", P._deps(L("qoffs"), []))
            qv = {}

            def load_q(e):
                e.reg_load(qreg, qoffs[0:1, 0:1])
                qv["v"] = e.snap(qreg, min_val=0, max_val=12)
                e.reg_load(qreg2, qoffs[0:1, 1:2])
                qv["f"] = e.snap(qreg2, min_val=0, max_val=6)

            P.raw("sync", load_q)
            P.dma(fnw[:], fnw_d.partition_broadcast(128), writes=L("fnw"))

            si = [0]

            def wload(dst_ap, src_ap, shape3, scale_col=None, wname="Wg"):
                sg = stg[si[0] % 2]
                si[0] += 1
                n = shape3[0] * shape3[1]
                view = sg[:, 0:n].rearrange("p (a n) -> p a n", n=shape3[1]) if shape3[0] > 1 else sg[:, 0:n]
                P.dma(view, src_ap, writes=L(sg.name))
                if scale_col is not None:
                    V(lambda e: e.tensor_scalar(dst_ap, view, scale_col, None, ALU.mult), [sg.name, "nw"],
                      [wname])
                elif si[0] % 2 == 0:
                    A(lambda e: e.activation(out=dst_ap, in_=view, func=AF.Copy), [sg.name], [wname])
                else:
                    V(lambda e: e.tensor_copy(dst_ap, view), [sg.name], [wname])

            for k in range(8):
                wload(Wg[:, k, :], wg_d[k * 128:(k + 1) * 128, :], (1, 2048), scale_col=nw[:, k:k + 1])
            wfo_v = wfo_d.rearrange("(a p) n -> p a n", p=128)
            wml_v = wml_d.rearrange("(a p) n -> p a n", p=128)
            wout_v = wout_d.rearrange("(a p) n -> p a n", p=128)
            for i in range(4):
                wload(Wfo[:, 2 * i:2 * i + 2, :], wfo_v[:, 2 * i:2 * i + 2, :], (2, 1024), wname="Wfo")
            for i in range(8):
                wload(Wml[:, 2 * i:2 * i + 2, :], wml_v[:, 2 * i:2 * i + 2, :], (2, 1024), wname="Wml")
            for i in range(4):
                wload(Wout[:, 2 * i:2 * i + 2, :], wout_v[:, 2 * i:2 * i + 2, :], (2, 1024), wname="Wout")

            uout_v = uout_d.rearrange("tg (a p) t -> p tg a t", p=128)
            ufout_v = ufout_d.rearrange("tg (a p) t -> p tg a t", p=128)
            g_i = [0]
            o_i = [0]
            def load3(tb):
                xs3 = stg[tb % 2]
                xsv = xs3[:].rearrange("p (k t) -> p k t", t=NT)
                P.dma(xsv, xTq_d.rearrange("(k p) t -> p k t", p=128)[:, :, tb * NT:(tb + 1) * NT],
                      writes=L(xs3.name))
                ut_ = utb[tb % 2]
                P.dma(None, None, reads=L(*["ufout%d" % tg for tg in range(8)]), writes=L(ut_.name),
                      fn=lambda e, tb=tb, ut_=ut_: e.dma_start(
                          out=ut_[:, 0:8, :].unsqueeze(1),
                          in_=ufout_v[:, bass.ds(qv["f"] + tb // 4, 1), :, (tb % 4) * NT:(tb % 4 + 1) * NT]))
                P.dma(None, None, reads=L(*["uout%d" % tg for tg in range(16)]), writes=L(ut_.name),
                      fn=lambda e, tb=tb, ut_=ut_: e.dma_start(
                          out=ut_[:, 8:24, :].unsqueeze(1),
                          in_=uout_v[:, bass.ds(qv["v"] + tb // 2, 1), :, (tb % 2) * NT:(tb % 2 + 1) * NT]))

            load3(0)
            for tb in range(NB3):
                X = xb3[tb % 2]
                R = R3[tb % 2]
                xs3 = stg[tb % 2]
                ut = utb[tb % 2]
                xsv = xs3[:].rearrange("p (k t) -> p k t", t=NT)
                if tb + 1 < NB3:
                    load3(tb + 1)
                A(lambda e, X=X, xsv=xsv: e.activation(out=X[:], in_=xsv, func=AF.Copy), [xs3.name], [X.name])
                G(lambda e, xsv=xsv: e.tensor_tensor(sq3[:], xsv, xsv, ALU.mult), [xs3.name], ["sq3"])
                P.mm([(lambda e, k=k: e.matmul(ps_s3[:, 0:NT], onesb[:], sq3[:, k, :], start=(k == 0), stop=(k == 7)))
                      for k in range(8)], reads=L("onesb", "sq3"), writes=L("ps_s3"))
                A(lambda e, R=R: e.activation(out=R[:], in_=ps_s3[:, 0:NT], func=AF.Sqrt, bias=EPS, scale=1.0 / D),
                  ["ps_s3"], [R.name])
                V(lambda e, R=R: e.reciprocal(R[:], R[:]), [R.name], [R.name])
                for gc in range(16):
                    pp = ps_g[g_i[0] % 2]
                    tf = tf3[g_i[0] % 2]
                    g_i[0] += 1
                    P.mm([(lambda e, k=k, gc=gc, pp=pp, X=X: e.matmul(pp[:, 0:NT], Wg[:, k, gc * 128:(gc + 1) * 128],
                                                                       X[:, k, :], start=(k == 0), stop=(k == 7)))
                          for k in range(8)], reads=L("Wg", X.name), writes=L(pp.name))
                    V(lambda e, pp=pp, tf=tf, R=R: e.tensor_tensor(tf[:], pp[:, 0:NT], R[:], ALU.mult),
                      [pp.name, R.name], [tf.name])
                    A(lambda e, gc=gc, tf=tf: e.activation(out=gT[:, gc, :], in_=tf[:], func=AF.Sigmoid),
                      [tf.name], ["gT"])
                for m in range(8):
                    P.mm([(lambda e, a=a, m=m, ut=ut: e.matmul(ps_a[:, 0:NT], Wfo[:, a, m * 128:(m + 1) * 128],
                                                        ut[:, a, :], start=(a == 0),
                                                        stop=(a == 7))) for a in range(8)],
                         reads=L("Wfo", ut.name), writes=L("ps_a"))
                    P.mm([(lambda e, a=a, m=m, ut=ut: e.matmul(ps_b[:, 0:NT], Wml[:, a, m * 128:(m + 1) * 128],
                                                        ut[:, 8 + a, :], start=(a == 0),
                                                        stop=(a == 15))) for a in range(16)],
                         reads=L("Wml", ut.name), writes=L("ps_b"))
                    V(lambda e, m=m: e.tensor_tensor(ta[:], ps_a[:, 0:NT], gT[:, m, :], ALU.mult),
                      ["ps_a", "gT"], ["ta"])
                    V(lambda e, m=m: e.tensor_tensor(tb_[:], ps_b[:, 0:NT], gT[:, 8 + m, :], ALU.mult),
                      ["ps_b", "gT"], ["tb_"])
                    G(lambda e, m=m: e.tensor_tensor(yT[:, m, :], ta[:], tb_[:], ALU.add), ["ta", "tb_"], ["yT"])
                for tt in range(NT // 128):
                    row0 = (tb * (NT // 128) + tt) * 128
                    P.dma(xres[:], xq_d[row0:row0 + 128, :], writes=L("xres"))
                    for half in range(2):
                        pp = ps_o[o_i[0] % 2]
                        o_i[0] += 1
                        P.mm([(lambda e, m=m, tt=tt, half=half, pp=pp: e.matmul(
                            pp[:], yT[:, m, tt * 128:(tt + 1) * 128], Wout[:, m, half * 512:(half + 1) * 512],
                            start=(m == 0), stop=(m == 7))) for m in range(8)],
                            reads=L("Wout", "yT"), writes=L(pp.name))
                        V(lambda e, half=half, pp=pp: e.tensor_tensor(xo[:, half * 512:(half + 1) * 512], pp[:],
                                                                      xres[:, half * 512:(half + 1) * 512], ALU.add),
                          [pp.name, "xres"], ["xo"])
                    A(lambda e: e.activation(out=junk3[:], in_=xo[:], func=AF.Square, accum_out=ssq3[:]),
                      ["xo"], ["junk3", "ssq3"])
                    A(lambda e: e.activation(out=ssq3[:], in_=ssq3[:], func=AF.Sqrt, bias=EPS, scale=1.0 / D),
                      ["ssq3"], ["ssq3"])
                    V(lambda e: e.reciprocal(ssq3[:], ssq3[:]), ["ssq3"], ["ssq3"])
                    V(lambda e: e.scalar_tensor_tensor(yo[:], xo[:], ssq3[:, 0:1], fnw[:], ALU.mult, ALU.mult),
                      ["xo", "ssq3", "fnw"], ["yo"])
                    P.dma(out_d[row0:row0 + 128, :], yo[:], reads=L("yo"), writes=L("out_d"))
            P.barrier()
        P.run()
    return nc


def _consts():
    c = {}
    c["identb"] = np.eye(128, dtype=np.float32).astype(ml_dtypes.bfloat16)
    c["identf"] = np.eye(128, dtype=np.float32)
    s = np.arange(128)[:, None]
    t = np.arange(128)[None, :]
    masks = np.zeros((128, 2, 128), np.float32)
    masks[:, 0, :] = np.where(s <= t, 0.0, 1.0e4)
    masks[:, 1, :] = np.where(s >= t, 0.0, 1.0e4)
    c["masks"] = masks
    cp = np.arange(64)[:, None]
    cc = np.arange(64)[None, :]
    ustr = np.zeros((64, 2, 64), np.float32)
    ustr[:, 0, :] = (cp < cc)
    ustr[:, 1, :] = (cp > cc)
    c["ustr"] = ustr
    scale = 1.0 / np.sqrt(8192.0 * 256.0)
    n = np.arange(256, dtype=np.float64)
    ang = 2.0 * np.pi * np.outer(n, n) / 256.0
    dftc = np.concatenate([np.cos(ang), -np.sin(ang)], axis=1) * scale
    c["dftc"] = np.ascontiguousarray(dftc.reshape(2, 128, 512).transpose(1, 0, 2)).astype(np.float32).astype(
        ml_dtypes.bfloat16)
    n1 = np.arange(128, dtype=np.float64)[:, None]
    k1 = np.arange(128, dtype=np.float64)[None, :]
    m1 = np.zeros((64, 128, 384), np.float64)
    for n2 in range(64):
        th = 2.0 * np.pi * k1 * (64.0 * n1 + n2) / 8192.0
        m1[n2, :, 0:128] = np.cos(th)
        m1[n2, :, 128:256] = np.sin(th)
        m1[n2, :, 256:384] = -np.sin(th)
    c["m1"] = m1.astype(np.float32).astype(ml_dtypes.bfloat16)
    n2 = np.arange(64, dtype=np.float64)[:, None]
    k2 = np.arange(64, dtype=np.float64)[None, :]
    ph = 2.0 * np.pi * n2 * k2 / 64.0
    c["cs2"] = np.concatenate([np.cos(ph), np.sin(ph)], axis=0).astype(np.float32).astype(ml_dtypes.bfloat16)
    return c


def prep_inputs(inp):
    f32 = np.float32
    x = np.asarray(inp["x"], f32)
    w_in = np.asarray(inp["w_in"], f32)[0]
    consts = _consts()
    xT = [np.ascontiguousarray(x[b].T) for b in range(2)]
    maps = []
    for core in range(8):
        b, g = core // 4, core % 4
        m = dict(consts)
        m["xT"] = xT[b]
        m["xTq"] = np.ascontiguousarray(xT[b][:, 2048 * g:2048 * (g + 1)])
        m["xq"] = np.ascontiguousarray(x[b, 2048 * g:2048 * (g + 1), :])
        cols = np.concatenate([np.arange(1024 + 256 * g, 1024 + 256 * (g + 1)),
                               np.arange(2048 + 512 * g, 2048 + 512 * (g + 1)),
                               np.arange(4096 + 512 * g, 4096 + 512 * (g + 1)),
                               np.arange(6144 + 512 * g, 6144 + 512 * (g + 1))])
        m["w_own"] = np.ascontiguousarray(w_in[:, cols])
        m["wfT"] = np.ascontiguousarray(w_in[:, 256 * g:256 * (g + 1)].T)
        m["wg"] = np.ascontiguousarray(w_in[:, 8192:10240])
        m["nw"] = np.ascontiguousarray(np.asarray(inp["norm_w"], f32)[0].reshape(8, 128).T)
        cw = np.asarray(inp["conv_w"], f32)[0][:, 512 * g:512 * (g + 1)]
        m["convw"] = np.ascontiguousarray(cw.reshape(5, 4, 128).transpose(2, 1, 0))
        m["convb"] = np.ascontiguousarray(np.asarray(inp["conv_b"], f32)[0][512 * g:512 * (g + 1)].reshape(4, 128).T)
        bd = np.zeros((128, 3, 4, 128), f32)
        bdT = np.zeros((128, 3, 4, 128), f32)
        for s_i, nm in enumerate(["w_q", "w_k", "w_v"]):
            w = np.asarray(inp[nm], f32)[0]
            for j in range(4):
                for bl in range(32):
                    blk = w[128 * g + 32 * j + bl]
                    bd[4 * bl:4 * bl + 4, s_i, j, 4 * bl:4 * bl + 4] = blk
                    bdT[4 * bl:4 * bl + 4, s_i, j, 4 * bl:4 * bl + 4] = blk.T
        m["bd"] = bd
        m["bdT"] = bdT
        gnames = ["w_igate_fwd", "w_fgate_fwd", "w_igate_bwd", "w_fgate_bwd"]
        wgate = np.zeros((128, 12, 16), f32)
        for t_i, nm in enumerate(gnames):
            w = np.asarray(inp[nm], f32)[0]
            for s_i in range(3):
                rows = w[2048 * s_i + 512 * g:2048 * s_i + 512 * (g + 1), :]
                wgate[:, 4 * s_i:4 * s_i + 4, 4 * t_i:4 * t_i + 4] = rows.reshape(4, 128, 4).transpose(1, 0, 2)
        m["wgate"] = wgate
        sel = np.zeros((64, 4), f32)
        sel[:, g] = 1.0
        m["sel"] = sel
        bnames = ["b_igate_fwd", "b_fgate_fwd", "b_igate_bwd", "b_fgate_bwd"]
        gb = np.zeros((64, 4), f32)
        for t_i, nm in enumerate(bnames):
            gb[:, t_i] = np.asarray(inp[nm], f32)[0][g]
        m["gb"] = gb
        m["hnw"] = np.ascontiguousarray(np.asarray(inp["hnorm_w"], f32)[0][512 * g:512 * (g + 1)].reshape(4, 128).T)
        m["skw"] = np.ascontiguousarray(np.asarray(inp["skip_w"], f32)[0][512 * g:512 * (g + 1)].reshape(4, 128).T)
        m["wfo"] = np.asarray(inp["w_fourier"], f32)[0]
        m["wml"] = np.asarray(inp["w_mlstm"], f32)[0]
        m["wout"] = np.asarray(inp["w_out"], f32)[0]
        m["fnw"] = np.asarray(inp["final_norm_w"], f32).reshape(1, 1024)
        m["qoff"] = np.array([[4 * g, 2 * g]], dtype=np.int32)
        maps.append({"i_" + k: v for k, v in m.items()})
    return maps


_NC_CACHE = {}


def kernel(**inputs):
    if "nc" not in _NC_CACHE:
        _NC_CACHE["nc"] = build_program()
    nc = _NC_CACHE["nc"]
    maps = prep_inputs(inputs)
    res = run_bass_kernel_spmd(nc, maps, core_ids=list(range(8)))
    out = np.zeros((2, S, D), np.float32)
    for core in range(8):
        b, g = core // 4, core % 4
        out[b, 2048 * g:2048 * (g + 1), :] = np.asarray(res.results[core]["out"], np.float32)
    return out
```
